# Optimizing a Trainium2 kernel written in Bass

```python
import jax, jax.numpy as jnp
from jax import lax
import numpy as np

D_MODEL = 2048
BATCH = 4
SEQ = 2048
DEPTH = 4
DEC_BATCH = 128
DEC_SEQ = 1
PAST_LEN = 16384
PAGE_SIZE = 128

MIX_WIDTH = D_MODEL
A_WIDTH = MIX_WIDTH // 2
B_WIDTH = MIX_WIDTH - A_WIDTH
A_HEADS = 8
A_HEAD_DIM = A_WIDTH // A_HEADS
A_CHUNK = 128
B_HEADS = 8
B_KEY_DIM = 128
B_VAL_DIM = B_WIDTH // B_HEADS
B_QF_WIDTH = B_HEADS * B_KEY_DIM
B_CHUNK = 64
D_FF = -(-8 * D_MODEL // (3 * 256)) * 256
IN_WIDTH = 2 * A_WIDTH + 2 * B_QF_WIDTH + 2 * B_WIDTH
SPLITS = [A_WIDTH, 2 * A_WIDTH, 2 * A_WIDTH + B_QF_WIDTH, 2 * A_WIDTH + 2 * B_QF_WIDTH,
          2 * A_WIDTH + 2 * B_QF_WIDTH + B_WIDTH]
EPS = 1e-6

kernel_name = "hybrid_gmlp_hgrn2_decode_step"


def rms_norm(x, g):
    x32 = x.astype(jnp.float32)
    y = x32 * lax.rsqrt(jnp.mean(x32 * x32, axis=-1, keepdims=True) + EPS)
    return (y * g.astype(jnp.float32)).astype(x.dtype)


def layer_norm(x, g, b):
    x32 = x.astype(jnp.float32)
    mu = jnp.mean(x32, axis=-1, keepdims=True)
    xc = x32 - mu
    y = xc * lax.rsqrt(jnp.mean(xc * xc, axis=-1, keepdims=True) + EPS)
    return (y * g.astype(jnp.float32) + b.astype(jnp.float32)).astype(x.dtype)


def chunk_spatial_gating(u, v, w_s, b_s):
    bn, L, _ = v.shape
    pad = (-L) % A_CHUNK
    n = (L + pad) // A_CHUNK
    vp = jnp.pad(v, ((0, 0), (0, pad), (0, 0))).reshape(bn, n, A_CHUNK, A_HEADS, A_HEAD_DIM)
    mask = jnp.tril(jnp.ones((A_CHUNK, A_CHUNK), dtype=bool))
    w = jnp.where(mask, w_s, 0.0).astype(v.dtype)
    s = jnp.einsum('hts,bnshd->bnthd', w, vp) + b_s.T[:, :, None].astype(v.dtype)
    s = s.reshape(bn, n * A_CHUNK, A_WIDTH)[:, :L]
    return u * s


def hgrn2_recurrence(q, z_f, i_in, S0, lb):
    bn, L = q.shape[0], q.shape[1]
    z = z_f.astype(jnp.float32)
    lb = lb.astype(jnp.float32)
    logf = jnp.logaddexp(jnp.log(lb), jnp.log1p(-lb) + jax.nn.log_sigmoid(z))
    k = (1.0 - lb) * jax.nn.sigmoid(-z)
    qf = jax.nn.silu(q.astype(jnp.float32))
    vf = i_in.astype(jnp.float32)
    C = min(B_CHUNK, L)
    pad = (-L) % C
    n = (L + pad) // C

    def to_chunks(a):
        a = jnp.pad(a, ((0, 0), (0, pad), (0, 0), (0, 0)))
        return jnp.moveaxis(a.reshape(bn, n, C, a.shape[2], a.shape[3]), 1, 0)

    qc, lfc, kc, vc = to_chunks(qf), to_chunks(logf), to_chunks(k), to_chunks(vf)
    mask = jnp.tril(jnp.ones((C, C), dtype=bool))[None, :, :, None, None]

    def step(S, inp):
        qb, lfb, kb, vb = inp
        bcum = jnp.cumsum(lfb, axis=1)
        o_inter = jnp.einsum('bthk,bhkv->bthv', qb * jnp.exp(bcum), S)
        diff = bcum[:, :, None] - bcum[:, None, :]
        decay = jnp.exp(jnp.where(mask, diff, -jnp.inf))
        attn = jnp.einsum('bthk,btshk,bshk->btsh', qb, decay, kb)
        o_intra = jnp.einsum('btsh,bshv->bthv', attn, vb)
        blast = bcum[:, -1]
        S_new = jnp.exp(blast)[..., None] * S + jnp.einsum(
            'bshk,bshv->bhkv', kb * jnp.exp(blast[:, None] - bcum), vb)
        return S_new, o_inter + o_intra

    S_fin, o = lax.scan(step, S0.astype(jnp.float32), (qc, lfc, kc, vc))
    o = jnp.moveaxis(o, 0, 1).reshape(bn, n * C, B_HEADS, B_VAL_DIM)[:, :L]
    return o, S_fin


def trunk_layer(x, S0, lb, g_mix_pre, g_mix_post, w_in, ln_g, ln_b, w_s, b_s,
                g_out, w_out, g_ffn_pre, g_ffn_post, w_gate, w_up, w_down):
    bn, L, _ = x.shape
    h = rms_norm(x, g_mix_pre)
    proj = h @ w_in.astype(x.dtype)
    u, v, q, zf, ib, g = jnp.split(proj, SPLITS, axis=-1)
    u = jax.nn.gelu(u, approximate=False)
    v = layer_norm(jax.nn.gelu(v, approximate=False), ln_g, ln_b)
    o_a = chunk_spatial_gating(u, v, w_s, b_s)
    v_rows = v[:, ((L - 1) // A_CHUNK) * A_CHUNK:]
    o_b, S_new = hgrn2_recurrence(q.reshape(bn, L, B_HEADS, B_KEY_DIM),
                                  zf.reshape(bn, L, B_HEADS, B_KEY_DIM),
                                  ib.reshape(bn, L, B_HEADS, B_VAL_DIM), S0, lb)
    o_b = o_b * lax.rsqrt(jnp.mean(o_b * o_b, axis=-1, keepdims=True) + EPS)
    o_b = o_b * g_out.astype(jnp.float32).reshape(B_HEADS, B_VAL_DIM)
    o_b = (o_b.reshape(bn, L, B_WIDTH) * jax.nn.silu(g.astype(jnp.float32))).astype(x.dtype)
    mix = jnp.concatenate([o_a, o_b], axis=-1) @ w_out.astype(x.dtype)
    x = x + rms_norm(mix, g_mix_post)
    h = rms_norm(x, g_ffn_pre)
    f = (jax.nn.silu(h @ w_gate.astype(x.dtype)) * (h @ w_up.astype(x.dtype))) @ w_down.astype(x.dtype)
    x = x + rms_norm(f, g_ffn_post)
    return x, S_new, v_rows


def setup_inputs(seed: int = 0) -> dict:
    key = jax.random.key(seed)
    ks = jax.random.split(key, 20)
    nrm = jax.random.normal
    f32 = jnp.float32
    return {
        "x_prompt": nrm(ks[0], (BATCH, SEQ, D_MODEL), f32),
        "x_sample": nrm(ks[1], (DEC_BATCH, DEC_SEQ, D_MODEL), f32),
        "state_hgrn": nrm(ks[2], (DEPTH, DEC_BATCH, B_HEADS, B_KEY_DIM, B_VAL_DIM), f32),
        "norm_mix_pre": 1.0 + 0.05 * nrm(ks[3], (DEPTH, D_MODEL), f32),
        "norm_mix_post": 1.0 + 0.05 * nrm(ks[4], (DEPTH, D_MODEL), f32),
        "w_in": nrm(ks[5], (DEPTH, D_MODEL, IN_WIDTH), f32) * D_MODEL ** -0.5,
        "ln_v_gain": 1.0 + 0.05 * nrm(ks[6], (DEPTH, A_WIDTH), f32),
        "ln_v_bias": 0.02 * nrm(ks[7], (DEPTH, A_WIDTH), f32),
        "spatial_w": nrm(ks[8], (DEPTH, A_HEADS, A_CHUNK, A_CHUNK), f32) * A_CHUNK ** -0.5,
        "spatial_b": 1.0 + 0.1 * nrm(ks[9], (DEPTH, A_HEADS, A_CHUNK), f32),
        "lb_param": 0.5 * nrm(ks[10], (DEPTH, B_QF_WIDTH), f32),
        "hgrn_out_gain": 1.0 + 0.05 * nrm(ks[11], (DEPTH, B_WIDTH), f32),
        "w_out": nrm(ks[12], (DEPTH, MIX_WIDTH, D_MODEL), f32) * MIX_WIDTH ** -0.5,
        "norm_ffn_pre": 1.0 + 0.05 * nrm(ks[13], (DEPTH, D_MODEL), f32),
        "norm_ffn_post": 1.0 + 0.05 * nrm(ks[14], (DEPTH, D_MODEL), f32),
        "w_gate": nrm(ks[15], (DEPTH, D_MODEL, D_FF), f32) * D_MODEL ** -0.5,
        "w_up": nrm(ks[16], (DEPTH, D_MODEL, D_FF), f32) * D_MODEL ** -0.5,
        "w_down": nrm(ks[17], (DEPTH, D_FF, D_MODEL), f32) * D_FF ** -0.5,
    }


def reference(x_prompt, x_sample, state_hgrn, norm_mix_pre, norm_mix_post, w_in,
              ln_v_gain, ln_v_bias, spatial_w, spatial_b, lb_param, hgrn_out_gain,
              w_out, norm_ffn_pre, norm_ffn_post, w_gate, w_up, w_down):
    lb_cum = jnp.cumsum(jax.nn.softmax(lb_param.astype(jnp.float32), axis=0), axis=0)
    lb_all = (lb_cum - lb_cum[0:1]).reshape(DEPTH, B_HEADS, B_KEY_DIM)

    def run(x, S_in):
        S_out, v_out = [], []
        for l in range(DEPTH):
            x, S_l, v_l = trunk_layer(
                x, S_in[l], lb_all[l], norm_mix_pre[l], norm_mix_post[l], w_in[l],
                ln_v_gain[l], ln_v_bias[l], spatial_w[l], spatial_b[l], hgrn_out_gain[l],
                w_out[l], norm_ffn_pre[l], norm_ffn_post[l], w_gate[l], w_up[l], w_down[l])
            S_out.append(S_l.astype(S_in.dtype))
            v_out.append(v_l)
        return x, jnp.stack(S_out), jnp.stack(v_out)

    S_zero = jnp.zeros((DEPTH, BATCH, B_HEADS, B_KEY_DIM, B_VAL_DIM), state_hgrn.dtype)
    y_prompt, state_hgrn_prompt, state_v_prompt = run(x_prompt, S_zero)
    y_sample, state_hgrn_sample, state_v_sample = run(x_sample, state_hgrn)
    return (y_prompt, y_sample, state_hgrn_prompt, state_hgrn_sample, state_v_prompt, state_v_sample)
```

```python
import numpy as np
from contextlib import ExitStack
import concourse.bass as bass
import concourse.mybir as mybir
from concourse.bass_utils import run_bass_kernel_spmd

F32 = mybir.dt.float32
BF16 = mybir.dt.bfloat16
AF = mybir.ActivationFunctionType
ALU = mybir.AluOpType

DEPTH = 4
D = 2048
KC = 16
NP_ = 256
NS = 4
NT = NP_ + NS
NG = 4
NTL = 2
NSLOT = 3
NWT = 108
DFF = 5632
EPS = 1e-6
NQ = 4
JQ = 11


class _Op:
    __slots__ = ("eng", "emit", "deps", "tok", "signal", "val")

    def __init__(self, eng, emit, deps, tok):
        self.eng, self.emit, self.deps, self.tok = eng, emit, deps, tok
        self.signal = False
        self.val = 0


class Sched:
    ENG = ["pe", "act", "dve", "pool", "sp"]

    def __init__(self):
        self.ops = {e: [] for e in self.ENG}
        self.lastw = {}
        self.readers = {}
        self.dma_count = {}
        self.dma_inc = {}

    def add(self, eng, emit, reads=(), writes=(), dma_key=None, inc=16):
        deps = set()
        for r in reads:
            t = self.lastw.get(r)
            if t is not None:
                deps.add(t)
        for w in writes:
            t = self.lastw.get(w)
            if t is not None and (t[0] == "dma" or t[0] != eng or dma_key is not None):
                deps.add(t)
            for t in self.readers.get(w, ()):
                if t[0] == "dma" or t[0] != eng or dma_key is not None:
                    deps.add(t)
        idx = len(self.ops[eng])
        if dma_key is not None:
            c = self.dma_count.get(dma_key, 0) + 1
            self.dma_count[dma_key] = c
            tok = ("dma", dma_key, inc * c, inc)
            self.dma_inc[dma_key] = inc
        else:
            tok = (eng, idx)
        if eng == "pe":
            deps = {d for d in deps if d[0] != "pe"}
        deps.discard(tok)
        op = _Op(eng, emit, deps, tok)
        self.ops[eng].append(op)
        for r in reads:
            lst = self.readers.setdefault(r, [])
            if tok[0] != "dma":
                lst[:] = [t for t in lst if t[0] != eng]
            lst.append(tok)
        for w in writes:
            self.lastw[w] = tok
            self.readers[w] = []
        return tok

    def emit_all(self, nc, st):
        for e in self.ENG:
            for op in self.ops[e]:
                for d in op.deps:
                    if d[0] != "dma":
                        self.ops[d[0]][d[1]].signal = True
        for e in self.ENG:
            c = 0
            for op in self.ops[e]:
                if op.signal:
                    c += 1
                op.val = c
        esem = {e: st.enter_context(nc.semaphore("s_" + e)) for e in self.ENG}
        dsem = {k: st.enter_context(nc.semaphore("d_" + str(k))) for k in self.dma_count}
        block = st.enter_context(nc.Block())
        engobj = {"pe": "tensor", "act": "scalar", "dve": "vector", "pool": "gpsimd", "sp": "sync"}

        def mk(e):
            def body(eng):
                waited = {}
                for op in self.ops[e]:
                    for d in sorted(op.deps, key=str):
                        if d[0] == "dma":
                            sem, v, key = dsem[d[1]], d[2], ("dma", d[1])
                        else:
                            sem, v, key = esem[d[0]], self.ops[d[0]][d[1]].val, d[0]
                        if waited.get(key, 0) >= v:
                            continue
                        waited[key] = v
                        eng.wait_ge(sem, v)
                    ins = op.emit(eng)
                    if op.tok[0] == "dma":
                        ins.then_inc(dsem[op.tok[1]], op.tok[3])
                    elif op.signal:
                        ins.then_inc(esem[e], 1)
                if e == "sp":
                    for k, c in self.dma_count.items():
                        eng.wait_ge(dsem[k], self.dma_inc[k] * c)
            return body

        for e in self.ENG:
            getattr(block, engobj[e])(mk(e))


def build_nc():
    nc = bass.Bass("TRN2", target_bir_lowering=False)
    dt = nc.dram_tensor
    xT = dt("xT", [128, KC, NG * NT], F32, kind="ExternalInput").ap()
    Sin = dt("Sin", [DEPTH, NG * NS, 8, 128, 128], F32, kind="ExternalInput").ap()
    wA = dt("wA", [DEPTH, 24, 128, KC * 256], F32, kind="ExternalInput").ap()
    wO = dt("wO", [DEPTH, 8, 128, KC * 256], F32, kind="ExternalInput").ap()
    wGU = dt("wGU", [DEPTH, 44, 128, KC * 256], F32, kind="ExternalInput").ap()
    wD = dt("wD", [DEPTH, NQ * 8, 128, JQ * 256], F32, kind="ExternalInput").ap()
    gains_d = dt("gains", [128, DEPTH * 4 * KC], F32, kind="ExternalInput").ap()
    lnp_d = dt("lnp", [DEPTH, 2, 1024], F32, kind="ExternalInput").ap()
    wsT_d = dt("wsT", [DEPTH, 128, 1024], F32, kind="ExternalInput").ap()
    bs_d = dt("bs", [DEPTH, 1, 1024], F32, kind="ExternalInput").ap()
    bsS_d = dt("bsS", [DEPTH, 1, 32], F32, kind="ExternalInput").ap()
    wd4_d = dt("wd4", [DEPTH, 4, 32], F32, kind="ExternalInput").ap()
    lbp_d = dt("lbp", [128, DEPTH * 8], F32, kind="ExternalInput").ap()
    gout_d = dt("gout", [128, DEPTH * 8], F32, kind="ExternalInput").ap()
    mask8_d = dt("mask8", [128, 1024], F32, kind="ExternalInput").ap()
    ident_d = dt("ident", [128, 128], F32, kind="ExternalInput").ap()
    id4_d = dt("id4", [4, 4 * 128], F32, kind="ExternalInput").ap()
    flag_d = dt("flag", [128, 2], F32, kind="ExternalInput").ap()

    yT = dt("yT", [128, KC, NG * NT], F32, kind="ExternalOutput").ap()
    Sp_out = dt("Sp_out", [DEPTH, 8, 128, 128], F32, kind="ExternalOutput").ap()
    Ss_out = dt("Ss_out", [DEPTH, NG * NS, 8, 128, 128], F32, kind="ExternalOutput").ap()
    vp_out = dt("vp_out", [DEPTH, 128, 1024], F32, kind="ExternalOutput").ap()
    vs_out = dt("vs_out", [DEPTH, NG * NS, 1024], F32, kind="ExternalOutput").ap()
    T1 = dt("T1", [DEPTH, 128, 1024], F32, kind="Internal").ap()
    cc_in = dt("cc_in", [2 * 128, 1024], F32, kind="Internal").ap()
    cc_out = dt("cc_out", [4 * 128, 1024], F32, kind="Internal").ap()
    wbf_l = [dt("wbf%d" % l, [NWT, 128, KC * 256], BF16, kind="Internal").ap() for l in range(DEPTH)]

    S = Sched()
    A = S.add
    with ExitStack() as st:
        def sb(name, shape, dtype):
            return st.enter_context(nc.sbuf_tensor(name, shape, dtype))

        def pst(name, shape, dtype):
            return st.enter_context(nc.psum_tensor(name, shape, dtype))

        def fl(t):
            return t[:, :, :].rearrange("p a b -> p (a b)")

        def RA(name, n=KC):
            return ["%s%d" % (name, i) for i in range(n)]

        x = sb("x", [128, KC, NT], F32)
        h = sb("h", [128, KC, NT], BF16)
        mixo = sb("mixo", [128, KC * NT], F32)
        mixo3 = mixo[:, :].rearrange("p (a b) -> p a b", b=NT)
        gv = mixo[:, 0:3 * 1024].rearrange("p (a b) -> p a b", b=1024)
        wsf = mixo[:, 0:1024]
        mix = sb("mix", [128, KC, NT], BF16)
        hid = sb("hid", [128, JQ, NT], BF16)
        wring = [sb("wr%d" % i, [128, KC * 256], BF16) for i in range(NSLOT)]
        lng = sb("lng", [128, 1024], F32)
        lnb = sb("lnb", [128, 1024], F32)
        v_tm = sb("v_tm", [128, 3, 1024], BF16)
        u2 = sb("u2", [128, 2, NT], BF16)
        qa = sb("qa", [128, 8, NT], BF16)
        qS = sb("qS", [128, 8, NS], F32)
        sga = sb("sga", [128, 8, NT], BF16)
        kka = sb("kka", [128, 8, NT], F32)
        logf_2 = [sb("logf_%d" % i, [128, NT], F32) for i in range(2)]
        fS = sb("fS", [128, 8, NS], F32)
        i_tm = sb("i_tm", [128, 3, 1024], BF16)
        bc_2 = [sb("bc_%d" % i, [128, 2, 128], F32) for i in range(2)]
        bc = bc_2[0]
        eb_2 = [sb("eb_%d" % i, [128, 2, 128], F32) for i in range(2)]
        eb = eb_2[0]
        E1_2 = [sb("E1_%d" % i, [128, 2, 128], F32) for i in range(2)]
        E1 = E1_2[0]
        E2_2 = [sb("E2_%d" % i, [128, 2, 128], F32) for i in range(2)]
        E2 = E2_2[0]
        E3_2 = [sb("E3_%d" % i, [128, 2, 128], F32) for i in range(2)]
        E3 = E3_2[0]
        E4_2 = [sb("E4_%d" % i, [128, 2, 128], F32) for i in range(2)]
        E4 = E4_2[0]
        E3l = sb("E3l", [128, 8, 2], F32)
        qea = sb("qea", [128, 8, NP_], BF16)
        kea = sb("kea", [128, 8, NP_], BF16)
        qga = sb("qga", [128, 8, NP_], BF16)
        kdT_2 = [sb("kdT_%d" % i, [128, 2, 128], BF16) for i in range(2)]
        kdT = kdT_2[0]
        kd_2 = [sb("kd_%d" % i, [128, 2, 128], BF16) for i in range(2)]
        kd = kd_2[0]
        Ama = sb("Ama", [128, 8, NP_], BF16)
        Ama4 = Ama[:, :, :].rearrange("p h (c t) -> p h c t", t=128)
        dSa = sb("dSa", [128, 8, NP_], F32)
        S0t = sb("S0t", [128, 8, 128], F32)
        S1t = sb("S1t", [128, 8, 128], F32)
        Sft = sb("Sft", [128, 8, 128], F32)
        Sct = sb("Sct", [128, 8, 128], F32)
        T1s = sb("T1s", [128, 8, 128], F32)
        S0b = sb("S0b", [128, 8, 128], BF16)
        S1b = sb("S1b", [128, 8, 128], BF16)
        Ssm = sb("Ssm", [128, NS, 2, 128], F32)
        kkS = sb("kkS", [4, 128], F32)
        kkm = sb("kkm", [4, NS, 128], BF16)
        sqb_2 = [sb("sqb_%d" % i, [128, NT], BF16) for i in range(2)]
        sqb = sqb_2[0]
        t1_2 = [sb("t1_%d" % i, [128, NT], F32) for i in range(2)]
        t1 = t1_2[0]
        rs_2 = [sb("rs_%d" % i, [128, NT], F32) for i in range(2)]
        rs = rs_2[0]
        stt = sb("stt", [128, 2, 6], F32)
        mv = sb("mv", [128, 2], F32)
        rsd = sb("rsd", [128, 1], F32)
        gains = sb("gains_s", [128, DEPTH * 4 * KC], F32)
        gout = sb("gout_s", [128, DEPTH * 8], F32)
        lbe = sb("lbe", [128, DEPTH * 8], F32)
        oml = sb("oml", [128, DEPTH * 8], F32)
        ltmp = sb("ltmp", [128, 8], F32)
        lr = sb("lr", [128, 8], F32)
        mask8 = sb("mask8_s", [128, 1024], F32)
        mask8v = mask8[:, :].rearrange("p (h t) -> p h t", t=128)
        ident = sb("ident_s", [128, 128], F32)
        identb = sb("identb", [128, 128], BF16)
        id4 = sb("id4_s", [4, NS, 128], F32)
        onesb = sb("onesb", [128, 128], BF16)
        onesf = sb("onesf", [128, 128], F32)
        flag = sb("flag_s", [128, 2], F32)
        wsb = sb("wsb", [128, 8, 128], BF16)
        bsf = sb("bsf", [1, 1024], F32)
        bsS = sb("bsS_s", [1, 32], F32)
        wd4f = sb("wd4f", [4, 32], F32)
        wd4b = sb("wd4b", [4, 8, 4], BF16)

        PD = [(pst("PD%d" % i, [128, 512], F32), "PD%d" % i) for i in range(4)]
        PC = pst("PC", [128, 1024], F32)
        PO_2 = [pst("PO%d" % i, [128, 512], F32) for i in range(2)]
        rot = [0]

        def next_ps():
            p = PD[rot[0] % 4]
            rot[0] += 1
            return p

        wstream = []
        for g in range(NG):
            for l in range(DEPTH):
                for t in range(24):
                    wstream.append((wA[l, t], KC * 256))
                for t in range(8):
                    wstream.append((wO[l, t], KC * 256))
                for qd in range(NQ):
                    for jj in range(JQ):
                        wstream.append((wGU[l, qd * JQ + jj], KC * 256))
                    for dp in range(8):
                        wstream.append((wD[l, qd * 8 + dp], JQ * 256))
        wstate = {"issued": 0, "taken": 0}

        def w_issue_upto(n):
            while wstate["issued"] < min(n, len(wstream)):
                i = wstate["issued"]
                src, ncols = wstream[i]
                sl = i % NSLOT
                j = i % (DEPTH * NWT)
                if i < DEPTH * NWT:
                    A("pool", lambda e, sl=sl, src=src, ncols=ncols: e.dma_start(out=wring[sl][:, 0:ncols], in_=src),
                      writes=["wr%d" % sl], dma_key="wr%d" % sl)
                    A("sp", lambda e, sl=sl, j=j, ncols=ncols: e.dma_start(out=wbf_l[j // NWT][j % NWT, :, 0:ncols], in_=wring[sl][:, 0:ncols]),
                      reads=["wr%d" % sl], writes=["wbf%d" % j], dma_key="ws%d" % sl)
                else:
                    A("sp", lambda e, sl=sl, j=j, ncols=ncols: e.dma_start(out=wring[sl][:, 0:ncols], in_=wbf_l[j // NWT][j % NWT, :, 0:ncols]),
                      reads=["wbf%d" % j], writes=["wr%d" % sl], dma_key="wr%d" % sl)
                wstate["issued"] += 1

        def w_take():
            i = wstate["taken"]
            wstate["taken"] += 1
            w_issue_upto(i + NSLOT)
            sl = i % NSLOT
            return wring[sl], "wr%d" % sl

        A("sp", lambda e: e.dma_start(out=gains[:], in_=gains_d), writes=["gains"], dma_key="c0")
        A("sp", lambda e: e.dma_start(out=gout[:], in_=gout_d), writes=["gout"], dma_key="c1")
        A("sp", lambda e: e.dma_start(out=lbe[:], in_=lbp_d), writes=["lbe"], dma_key="c2")
        A("sp", lambda e: e.dma_start(out=mask8[:], in_=mask8_d), writes=["mask8"], dma_key="c3")
        A("sp", lambda e: e.dma_start(out=ident[:], in_=ident_d), writes=["ident"], dma_key="c4")
        A("sp", lambda e: e.dma_start(out=fl(id4), in_=id4_d), writes=["id4"], dma_key="c5")
        A("sp", lambda e: e.dma_start(out=flag[:], in_=flag_d), writes=["flag"], dma_key="c6")
        A("dve", lambda e: e.memset(onesb[:], 1.0), writes=["onesb"])
        A("dve", lambda e: e.memset(onesf[:], 1.0), writes=["onesf"])
        A("dve", lambda e: e.memset(fl(Ama), 0.0), writes=["Ama"])
        A("dve", lambda e: e.memset(fl(Sft), 0.0), writes=["Sft"])
        A("dve", lambda e: e.tensor_copy(out=identb[:], in_=ident[:]), reads=["ident"], writes=["identb"])
        for l in range(DEPTH):
            A("sp", lambda e, l=l: e.dma_start(out=T1[l], in_=fl(Sft)), reads=["Sft"], writes=["T1_%d" % l], dma_key="t1w")
        A("act", lambda e: e.activation(out=lbe[:], in_=lbe[:], func=AF.Exp), reads=["lbe"], writes=["lbe"])
        A("dve", lambda e: e.tensor_tensor(out=ltmp[:], in0=lbe[:, 0:8], in1=lbe[:, 8:16], op=ALU.add), reads=["lbe"], writes=["ltmp"])
        A("dve", lambda e: e.tensor_tensor(out=ltmp[:], in0=ltmp[:], in1=lbe[:, 16:24], op=ALU.add), reads=["lbe", "ltmp"], writes=["ltmp"])
        A("dve", lambda e: e.tensor_tensor(out=ltmp[:], in0=ltmp[:], in1=lbe[:, 24:32], op=ALU.add), reads=["lbe", "ltmp"], writes=["ltmp"])
        A("dve", lambda e: e.reciprocal(out=lr[:], in_=ltmp[:]), reads=["ltmp"], writes=["lr"])
        A("dve", lambda e: e.memset(oml[:, 0:8], 1.0), writes=["oml"])
        A("dve", lambda e: e.tensor_copy(out=ltmp[:], in_=lbe[:, 8:16]), reads=["lbe", "lr"], writes=["ltmp"])
        for l in range(1, DEPTH):
            if l > 1:
                A("dve", lambda e, l=l: e.tensor_tensor(out=ltmp[:], in0=ltmp[:], in1=lbe[:, 8 * l:8 * l + 8], op=ALU.add),
                  reads=["lbe", "ltmp", "oml"], writes=["ltmp"])
            A("dve", lambda e, l=l: e.tensor_tensor(out=oml[:, 8 * l:8 * l + 8], in0=ltmp[:], in1=lr[:], op=ALU.mult),
              reads=["ltmp", "lr"], writes=["oml"])
            A("dve", lambda e, l=l: e.tensor_scalar(out=oml[:, 8 * l:8 * l + 8], in0=oml[:, 8 * l:8 * l + 8], scalar1=-1.0, scalar2=1.0,
                                                  op0=ALU.mult, op1=ALU.add), reads=["oml"], writes=["oml"])

        def rstd_from(ps, pname, scale, n, rs_t=None, rs_n="rs_0"):
            rt = rs if rs_t is None else rs_t
            A("act", lambda e: e.activation(out=rt[:, 0:n], in_=ps[:, 0:n], func=AF.Sqrt, scale=scale, bias=EPS), reads=[pname], writes=[rs_n])
            A("dve", lambda e: e.reciprocal(out=rt[:, 0:n], in_=rt[:, 0:n]), reads=[rs_n], writes=[rs_n])

        def colsum_h_to_rs(nchunks, scale):
            ps, pn = next_ps()
            for kc in range(nchunks):
                A("pe", lambda e, kc=kc: e.matmul(ps[:, 0:NT], lhsT=onesb[:], rhs=h[:, kc, :], start=(kc == 0), stop=(kc == nchunks - 1)),
                  reads=["h%d" % kc, "onesb"], writes=[pn])
            rstd_from(ps, pn, scale, NT)

        def on_pool(kc):
            return False

        def prenorm(l, nidx):
            for kc in range(KC):
                A("act", lambda e, kc=kc: e.activation(out=h[:, kc, :], in_=x[:, kc, :], func=AF.Square), reads=["x%d" % kc], writes=["h%d" % kc])
            colsum_h_to_rs(KC, 1.0 / D)
            for kc in range(KC):
                gi = (l * 4 + nidx) * KC + kc
                if on_pool(kc):
                    A("pool", lambda e, kc=kc: e.tensor_tensor(out=tmpP[:], in0=x[:, kc, :], in1=rs[:, 0:NT], op=ALU.mult), reads=["x%d" % kc, "rs_0"], writes=["tmpP"])
                    A("pool", lambda e, kc=kc, gi=gi: e.tensor_scalar(out=h[:, kc, :], in0=tmpP[:], scalar1=gains[:, gi:gi + 1], scalar2=None, op0=ALU.mult),
                      reads=["tmpP", "gains"], writes=["h%d" % kc])
                else:
                    A("dve", lambda e, kc=kc, gi=gi: e.scalar_tensor_tensor(out=h[:, kc, :], in0=x[:, kc, :], scalar=gains[:, gi:gi + 1], in1=rs[:, 0:NT],
                                                                            op0=ALU.mult, op1=ALU.mult), reads=["x%d" % kc, "rs_0", "gains"], writes=["h%d" % kc])

        def postnorm_residual(l, nidx):
            colsum_h_to_rs(KC, 1.0 / D)
            for kc in range(KC):
                gi = (l * 4 + nidx) * KC + kc
                if on_pool(kc):
                    A("pool", lambda e, kc=kc, gi=gi: e.tensor_scalar(out=mixo3[:, kc, :], in0=mixo3[:, kc, :], scalar1=gains[:, gi:gi + 1], scalar2=None, op0=ALU.mult),
                      reads=["mixo%d" % kc, "gains"], writes=["mixo%d" % kc])
                    A("pool", lambda e, kc=kc: e.tensor_tensor(out=mixo3[:, kc, :], in0=mixo3[:, kc, :], in1=rs[:, 0:NT], op=ALU.mult),
                      reads=["mixo%d" % kc, "rs_0"], writes=["mixo%d" % kc])
                    A("pool", lambda e, kc=kc: e.tensor_tensor(out=x[:, kc, :], in0=x[:, kc, :], in1=mixo3[:, kc, :], op=ALU.add),
                      reads=["mixo%d" % kc, "x%d" % kc], writes=["x%d" % kc])
                else:
                    A("dve", lambda e, kc=kc, gi=gi: e.scalar_tensor_tensor(out=mixo3[:, kc, :], in0=mixo3[:, kc, :], scalar=gains[:, gi:gi + 1], in1=rs[:, 0:NT],
                                                                            op0=ALU.mult, op1=ALU.mult), reads=["mixo%d" % kc, "rs_0", "gains"], writes=["mixo%d" % kc])
                    A("dve", lambda e, kc=kc: e.tensor_tensor(out=x[:, kc, :], in0=x[:, kc, :], in1=mixo3[:, kc, :], op=ALU.add),
                      reads=["mixo%d" % kc, "x%d" % kc], writes=["x%d" % kc])

        def mm_fm(ps, pn, wt, wn, wcol, act, an, nk, wstride=256):
            for kc in range(nk):
                A("pe", lambda e, kc=kc: e.matmul(ps[:, 0:NT], lhsT=wt[:, kc * wstride + wcol: kc * wstride + wcol + 128],
                                                  rhs=act[:, kc, :], start=(kc == 0), stop=(kc == nk - 1)),
                  reads=[wn, "%s%d" % (an, kc)], writes=[pn])

        def mm_tm(ps, pn, wt, wn, tt):
            a, b = (tt * 128, tt * 128 + 128) if tt < NTL else (NP_, NT)
            nt = b - a
            for kc in range(KC):
                A("pe", lambda e, kc=kc, a=a, b=b, nt=nt: e.matmul(ps[0:nt, 0:256], lhsT=h[:, kc, a:b], rhs=wt[:, kc * 256:(kc + 1) * 256],
                                                                     start=(kc == 0), stop=(kc == KC - 1)), reads=[wn, "h%d" % kc], writes=[pn])
            return nt

        def state_apply(src, srcn, dst1, dst1n, dstf, dstfn):
            for hd in range(8):
                A("dve", lambda e, hd=hd: e.scalar_tensor_tensor(out=dst1[:, hd, :], in0=src[:, hd, :], scalar=E3l[:, hd, 0:1], in1=dSa[:, hd, 0:128],
                                                               op0=ALU.mult, op1=ALU.add), reads=[srcn, "E3l", "dSa"], writes=[dst1n])
                A("dve", lambda e, hd=hd: e.scalar_tensor_tensor(out=dstf[:, hd, :], in0=dst1[:, hd, :], scalar=E3l[:, hd, 1:2], in1=dSa[:, hd, 128:256],
                                                               op0=ALU.mult, op1=ALU.add), reads=[dst1n, "E3l", "dSa"], writes=[dstfn])

        slot = 0
        for g in range(NG):
            c0 = g * NT
            A("sp", lambda e, c0=c0: e.dma_start(out=x[:, :, :], in_=xT[:, :, c0:c0 + NT]), writes=RA("x"), dma_key="xin")
            for l in range(DEPTH):
                last_g = (g == NG - 1)
                A("sp", lambda e, l=l: e.dma_start(out=lng[:], in_=lnp_d[l, 0:1, :].partition_broadcast(128)), writes=["lng"], dma_key="p0")
                A("sp", lambda e, l=l: e.dma_start(out=lnb[:], in_=lnp_d[l, 1:2, :].partition_broadcast(128)), writes=["lnb"], dma_key="p1")
                A("sp", lambda e, l=l: e.dma_start(out=wsf, in_=wsT_d[l]), writes=RA("mixo"), dma_key="p2")
                A("sp", lambda e, l=l: e.dma_start(out=bsf[:], in_=bs_d[l]), writes=["bsf"], dma_key="p3")
                A("sp", lambda e, l=l: e.dma_start(out=bsS[:], in_=bsS_d[l]), writes=["bsS"], dma_key="p4")
                A("sp", lambda e, l=l: e.dma_start(out=wd4f[:], in_=wd4_d[l]), writes=["wd4f"], dma_key="p5")
                A("dve", lambda e: e.tensor_tensor(out=fl(wsb), in0=wsf, in1=mask8[:], op=ALU.mult), reads=RA("mixo") + ["mask8"], writes=["wsb"])
                A("dve", lambda e: e.tensor_copy(out=fl(wd4b), in_=wd4f[:]), reads=["wd4f"], writes=["wd4b"])
                A("sp", lambda e, l=l: e.dma_start(out=fl(T1s), in_=T1[l]), reads=["T1_%d" % l], writes=["T1s"], dma_key="t1r")

                prenorm(l, 0)

                for j in range(4):
                    wt, wn = w_take()
                    for tt in range(NTL + 1):
                        nt = mm_tm(PC, "PC", wt, wn, tt)
                        A("act", lambda e, tt=tt, nt=nt, j=j: e.activation(out=gv[0:nt, tt, j * 256:(j + 1) * 256], in_=PC[0:nt, 0:256], func=AF.Gelu),
                          reads=["PC"], writes=RA("mixo"))
                for tt in range(NTL + 1):
                    nt = 128 if tt < NTL else NS
                    for hh in range(2):
                        A("dve", lambda e, tt=tt, nt=nt, hh=hh: e.bn_stats(out=stt[0:nt, hh, :], in_=gv[0:nt, tt, hh * 512:(hh + 1) * 512]),
                          reads=RA("mixo"), writes=["stt"])
                    A("dve", lambda e, nt=nt: e.bn_aggr(out=mv[0:nt, :], in_=stt[0:nt, :, :].rearrange("p a b -> p (a b)")), reads=["stt"], writes=["mv"])
                    A("act", lambda e, nt=nt: e.activation(out=rsd[0:nt, :], in_=mv[0:nt, 1:2], func=AF.Sqrt, bias=EPS), reads=["mv"], writes=["rsd"])
                    A("dve", lambda e, nt=nt: e.reciprocal(out=rsd[0:nt, :], in_=rsd[0:nt, :]), reads=["rsd"], writes=["rsd"])
                    A("dve", lambda e, tt=tt, nt=nt: e.tensor_scalar(out=gv[0:nt, tt, :], in0=gv[0:nt, tt, :], scalar1=mv[0:nt, 0:1], scalar2=rsd[0:nt, 0:1],
                                                                      op0=ALU.subtract, op1=ALU.mult), reads=RA("mixo") + ["mv", "rsd"], writes=RA("mixo"))
                    A("dve", lambda e, tt=tt, nt=nt: e.tensor_tensor(out=gv[0:nt, tt, :], in0=gv[0:nt, tt, :], in1=lng[0:nt, :], op=ALU.mult),
                      reads=RA("mixo") + ["lng"], writes=RA("mixo"))
                    A("dve", lambda e, tt=tt, nt=nt: e.tensor_tensor(out=gv[0:nt, tt, :], in0=gv[0:nt, tt, :], in1=lnb[0:nt, :], op=ALU.add),
                      reads=RA("mixo") + ["lnb"], writes=RA("mixo"))
                    A("act", lambda e, tt=tt, nt=nt: e.activation(out=v_tm[0:nt, tt, :], in_=gv[0:nt, tt, :], func=AF.Copy), reads=RA("mixo"), writes=["v_tm"])
                    if tt == NTL - 1 and last_g:
                        A("sp", lambda e, l=l, tt=tt: e.dma_start(out=vp_out[l], in_=gv[:, tt, :]), reads=RA("mixo"), dma_key="vpo")
                    if tt == NTL:
                        A("sp", lambda e, l=l, g=g, tt=tt: e.dma_start(out=vs_out[l, g * NS:(g + 1) * NS, :], in_=gv[0:NS, tt, :]), reads=RA("mixo"), dma_key="vso")

                for hp in range(4):
                    wt, wn = w_take()
                    for m in range(2):
                        hd = 2 * hp + m
                        ps, pn = next_ps()
                        mm_fm(ps, pn, wt, wn, m * 128, h, "h", KC)
                        A("act", lambda e, ps=ps, hd=hd: e.activation(out=qa[:, hd, :], in_=ps[:, 0:NT], func=AF.Silu), reads=[pn], writes=["qa"])
                        A("act", lambda e, ps=ps, hd=hd: e.activation(out=qS[:, hd, :], in_=ps[:, NP_:NT], func=AF.Silu), reads=[pn], writes=["qS"])
                    wt, wn = w_take()
                    for m in range(2):
                        hd = 2 * hp + m
                        ps, pn = next_ps()
                        mm_fm(ps, pn, wt, wn, m * 128, h, "h", KC)
                        A("act", lambda e, ps=ps, hd=hd: e.activation(out=sga[:, hd, :], in_=ps[:, 0:NT], func=AF.Silu), reads=[pn], writes=["sga"])
                    wt, wn = w_take()
                    for m in range(2):
                        hd = 2 * hp + m
                        ps, pn = next_ps()
                        mm_fm(ps, pn, wt, wn, m * 128, h, "h", KC)
                        A("act", lambda e, ps=ps, hd=hd: e.activation(out=kka[:, hd, :], in_=ps[:, 0:NT], func=AF.Sigmoid, scale=-1.0), reads=[pn], writes=["kka"])
                        oi = l * 8 + hd
                        A("dve", lambda e, hd=hd, oi=oi: e.tensor_scalar(out=kka[:, hd, :], in0=kka[:, hd, :], scalar1=oml[:, oi:oi + 1], scalar2=None, op0=ALU.mult),
                          reads=["kka", "oml"], writes=["kka"])
                        A("dve", lambda e, hd=hd: e.tensor_scalar(out=fS[:, hd, :], in0=kka[:, hd, NP_:NT], scalar1=-1.0, scalar2=1.0, op0=ALU.mult, op1=ALU.add),
                          reads=["kka"], writes=["fS"])
                    wt, wn = w_take()
                    for tt in range(NTL + 1):
                        nt = mm_tm(PC, "PC", wt, wn, tt)
                        A("act", lambda e, tt=tt, nt=nt, hp=hp: e.activation(out=i_tm[0:nt, tt, hp * 256:(hp + 1) * 256], in_=PC[0:nt, 0:256], func=AF.Copy),
                          reads=["PC"], writes=["i_tm"])

                for hd in range(8):
                    p2 = hd % 2
                    bc, eb, E1, E2, E3, E4, kdT, kd = bc_2[p2], eb_2[p2], E1_2[p2], E2_2[p2], E3_2[p2], E4_2[p2], kdT_2[p2], kd_2[p2]
                    sx = "_%d" % p2
                    logf = logf_2[p2]
                    A("act", lambda e, hd=hd, logf=logf: e.activation(out=logf[:], in_=kka[:, hd, :], func=AF.Ln, scale=-1.0, bias=1.0), reads=["kka"], writes=["logf" + sx])
                    for c in range(NTL):
                        cs = slice(c * 128, (c + 1) * 128)
                        A("dve", lambda e, c=c, cs=cs, bc=bc, logf=logf: e.tensor_tensor_scan(out=bc[:, c, :], data0=onesf[:], data1=logf[:, cs], initial=0.0,
                                                                                             op0=ALU.mult, op1=ALU.add), reads=["logf" + sx, "onesf"], writes=["bc" + sx])
                    for c in range(NTL):
                        A("dve", lambda e, c=c, bc=bc, eb=eb: e.tensor_scalar(out=eb[:, c, :], in0=bc[:, c, :], scalar1=bc[:, c, 63:64], scalar2=-80.0, op0=ALU.subtract, op1=ALU.max),
                          reads=["bc" + sx], writes=["eb" + sx])
                    A("dve", lambda e, eb=eb: e.tensor_scalar(out=fl(eb), in0=fl(eb), scalar1=80.0, scalar2=None, op0=ALU.min), reads=["eb" + sx], writes=["eb" + sx])
                    A("act", lambda e, E1=E1, eb=eb: e.activation(out=fl(E1), in_=fl(eb), func=AF.Exp), reads=["eb" + sx], writes=["E1" + sx])
                    A("act", lambda e, E2=E2, eb=eb: e.activation(out=fl(E2), in_=fl(eb), func=AF.Exp, scale=-1.0), reads=["eb" + sx], writes=["E2" + sx])
                    A("act", lambda e, E3=E3, bc=bc: e.activation(out=fl(E3), in_=fl(bc), func=AF.Exp), reads=["bc" + sx], writes=["E3" + sx])
                    for c in range(NTL):
                        A("act", lambda e, c=c, E4=E4, bc=bc: e.activation(out=E4[:, c, :], in_=bc[:, c, :], func=AF.Exp, scale=-1.0, bias=bc[:, c, 127:128]),
                          reads=["bc" + sx], writes=["E4" + sx])
                    A("dve", lambda e, hd=hd, E1=E1: e.tensor_tensor(out=qea[:, hd, :], in0=qa[:, hd, 0:NP_], in1=fl(E1), op=ALU.mult), reads=["qa", "E1" + sx], writes=["qea%d" % hd])
                    A("dve", lambda e, hd=hd, E2=E2: e.tensor_tensor(out=kea[:, hd, :], in0=kka[:, hd, 0:NP_], in1=fl(E2), op=ALU.mult), reads=["kka", "E2" + sx], writes=["kea%d" % hd])
                    A("dve", lambda e, hd=hd, E3=E3: e.tensor_tensor(out=qga[:, hd, :], in0=qa[:, hd, 0:NP_], in1=fl(E3), op=ALU.mult), reads=["qa", "E3" + sx], writes=["qga%d" % hd])
                    A("dve", lambda e, hd=hd, E4=E4, kdT=kdT: e.tensor_tensor(out=fl(kdT), in0=kka[:, hd, 0:NP_], in1=fl(E4), op=ALU.mult), reads=["kka", "E4" + sx], writes=["kdT" + sx])
                    A("act", lambda e, hd=hd, E3=E3: e.activation(out=E3l[:, hd, :], in_=E3[:, :, 127], func=AF.Copy), reads=["E3" + sx], writes=["E3l"])
                    pk, pkn = next_ps()
                    for c in range(NTL):
                        A("pe", lambda e, c=c, pk=pk, kdT=kdT: e.matmul(pk[:, c * 128:(c + 1) * 128], lhsT=kdT[:, c, :], rhs=identb[:], start=True, stop=True),
                          reads=["kdT" + sx, "identb"], writes=[pkn])
                    A("act", lambda e, pk=pk, kd=kd: e.activation(out=fl(kd), in_=pk[:, 0:NP_], func=AF.Copy), reads=[pkn], writes=["kd" + sx])
                    pa, pan = next_ps()
                    for c in range(NTL):
                        o0 = c * 128
                        A("pe", lambda e, hd=hd, o0=o0, pa=pa: e.matmul(pa[0:64, o0:o0 + 64], lhsT=kea[:, hd, o0:o0 + 64], rhs=qea[:, hd, o0:o0 + 64], start=True, stop=True),
                          reads=["kea%d" % hd, "qea%d" % hd], writes=[pan])
                        A("pe", lambda e, hd=hd, o0=o0, pa=pa: e.matmul(pa[:, o0 + 64:o0 + 128], lhsT=kea[:, hd, o0:o0 + 128], rhs=qea[:, hd, o0 + 64:o0 + 128], start=True, stop=True),
                          reads=["kea%d" % hd, "qea%d" % hd], writes=[pan])
                    A("dve", lambda e, hd=hd, pa=pa: e.tensor_tensor(out=Ama4[0:64, hd, :, 0:64], in0=pa[0:64, 0:NP_].rearrange("p (c t) -> p c t", t=128)[:, :, 0:64],
                                                                      in1=mask8v[0:64, 0:NTL, 0:64], op=ALU.mult), reads=[pan, "mask8"], writes=["Ama%d" % hd])
                    A("dve", lambda e, hd=hd, pa=pa: e.tensor_tensor(out=Ama4[:, hd, :, 64:128], in0=pa[:, 0:NP_].rearrange("p (c t) -> p c t", t=128)[:, :, 64:128],
                                                                      in1=mask8v[:, 0:NTL, 64:128], op=ALU.mult), reads=[pan, "mask8"], writes=["Ama%d" % hd])
                    pd_, pdn = next_ps()
                    for c in range(NTL):
                        A("pe", lambda e, hd=hd, c=c, pd_=pd_, kd=kd: e.matmul(pd_[:, c * 128:(c + 1) * 128], lhsT=kd[:, c, :], rhs=i_tm[:, c, hd * 128:(hd + 1) * 128], start=True, stop=True),
                          reads=["kd" + sx, "i_tm"], writes=[pdn])
                    A("act", lambda e, hd=hd, pd_=pd_: e.activation(out=dSa[:, hd, :], in_=pd_[:, 0:NP_], func=AF.Copy), reads=[pdn], writes=["dSa"])

                A("sp", lambda e: e.dma_start(out=cc_in[128:256, :], in_=fl(Sft)), reads=["Sft"], writes=["cc_in"], dma_key="cci1")
                state_apply(T1s, "T1s", S1t, "S1t", Sct, "Sct")
                A("sp", lambda e: e.dma_start(out=cc_in[0:128, :], in_=fl(Sct)), reads=["Sct"], writes=["cc_in"], dma_key="cci0")
                A("pool", lambda e: e.collective_compute("AllGather", ALU.bypass, replica_groups=[[0, 1], [2, 3], [4, 5], [6, 7]], ins=[cc_in], outs=[cc_out]),
                  reads=["cc_in"], writes=["cc_out"], dma_key="cc", inc=1)
                A("sp", lambda e: e.dma_start(out=fl(Sct), in_=cc_out[0:128, :]), reads=["cc_out"], writes=["Sct"], dma_key="cco0")
                if slot > 0:
                    pl = (slot - 1) % DEPTH
                    A("sp", lambda e, pl=pl: e.dma_start(out=T1[pl], in_=cc_out[384:512, :]), reads=["cc_out"], writes=["T1_%d" % pl], dma_key="t1w")
                slot += 1

                for hp in range(4):
                    wt, wn = w_take()
                    for m in range(2):
                        hd = 2 * hp + m
                        ps, pn = next_ps()
                        mm_fm(ps, pn, wt, wn, m * 128, h, "h", KC)
                        A("act", lambda e, ps=ps, m=m: e.activation(out=u2[:, m, :], in_=ps[:, 0:NT], func=AF.Gelu), reads=[pn], writes=["u2"])
                        ps2, pn2 = next_ps()
                        for c in range(NTL):
                            A("pe", lambda e, c=c, hd=hd, ps2=ps2: e.matmul(ps2[:, c * 128:(c + 1) * 128], lhsT=v_tm[:, c, hd * 128:(hd + 1) * 128], rhs=wsb[:, hd, :],
                                                                            start=True, stop=False), reads=["v_tm", "wsb"], writes=[pn2])
                            A("pe", lambda e, c=c, hd=hd, ps2=ps2: e.matmul(ps2[:, c * 128:(c + 1) * 128], lhsT=onesf[0:1, :], rhs=bsf[0:1, hd * 128:(hd + 1) * 128],
                                                                            start=False, stop=True), reads=["onesf", "bsf"], writes=[pn2])
                        A("pe", lambda e, hd=hd, ps2=ps2: e.matmul(ps2[:, NP_:NT], lhsT=v_tm[0:NS, NTL, hd * 128:(hd + 1) * 128], rhs=wd4b[:, hd, :],
                                                                   start=True, stop=False), reads=["v_tm", "wd4b"], writes=[pn2])
                        A("pe", lambda e, hd=hd, ps2=ps2: e.matmul(ps2[:, NP_:NT], lhsT=onesf[0:1, :], rhs=bsS[0:1, hd * 4:(hd + 1) * 4],
                                                                   start=False, stop=True), reads=["onesf", "bsS"], writes=[pn2])
                        A("dve", lambda e, m=m, hd=hd, ps2=ps2: e.tensor_tensor(out=mix[:, hd, :], in0=u2[:, m, :], in1=ps2[:, 0:NT], op=ALU.mult),
                          reads=["u2", pn2], writes=["mix%d" % hd])

                A("dve", lambda e: e.tensor_scalar(out=fl(S0t), in0=fl(T1s), scalar1=flag[:, 0:1], scalar2=None, op0=ALU.mult), reads=["T1s", "flag"], writes=["S0t"])
                A("dve", lambda e: e.scalar_tensor_tensor(out=fl(S0t), in0=fl(Sct), scalar=flag[:, 1:2], in1=fl(S0t), op0=ALU.mult, op1=ALU.add),
                  reads=["Sct", "flag", "S0t"], writes=["S0t"])
                state_apply(S0t, "S0t", S1t, "S1t", Sft, "Sft")
                A("act", lambda e: e.activation(out=fl(S0b), in_=fl(S0t), func=AF.Copy), reads=["S0t"], writes=["S0b"])
                A("act", lambda e: e.activation(out=fl(S1b), in_=fl(S1t), func=AF.Copy), reads=["S1t"], writes=["S1b"])
                if last_g:
                    A("sp", lambda e, l=l: e.dma_start(out=Sp_out[l].rearrange("h k v -> k h v"), in_=Sft[:, :, :]), reads=["Sft"], dma_key="spo")

                for hp in range(4):
                    for n in range(NS):
                        sn = g * NS + n
                        A("sp", lambda e, n=n, sn=sn, l=l, hp=hp: e.dma_start(out=Ssm[:, n, :, :], in_=Sin[l, sn, 2 * hp:2 * hp + 2].rearrange("h k v -> k h v")),
                          writes=["Ssm%d" % n], dma_key="ssl%d" % n)
                    for m in range(2):
                        hd = 2 * hp + m
                        PO, POn = PO_2[m], "PO%d" % m
                        sqb, t1, rs_, t1n, sqbn, rsn = sqb_2[m], t1_2[m], rs_2[m], "t1_%d" % m, "sqb_%d" % m, "rs_%d" % m
                        for c in range(NTL):
                            cs = slice(c * 128, (c + 1) * 128)
                            Sb, Sbn = (S0b, "S0b") if c == 0 else (S1b, "S1b")
                            A("pe", lambda e, c=c, hd=hd, cs=cs, PO=PO: e.matmul(PO[:, cs], lhsT=i_tm[:, c, hd * 128:(hd + 1) * 128], rhs=Ama[:, hd, cs], start=True, stop=False),
                              reads=["i_tm", "Ama%d" % hd], writes=[POn])
                            A("pe", lambda e, hd=hd, cs=cs, Sb=Sb, PO=PO: e.matmul(PO[:, cs], lhsT=Sb[:, hd, :], rhs=qga[:, hd, cs], start=False, stop=True),
                              reads=[Sbn, "qga%d" % hd], writes=[POn])
                        A("pe", lambda e, hd=hd: e.matmul(PC[0:NS, 0:128], lhsT=kka[:, hd, NP_:NT], rhs=ident[:], start=True, stop=True), reads=["kka", "ident"], writes=["PC"])
                        A("act", lambda e: e.activation(out=kkS[:], in_=PC[0:NS, 0:128], func=AF.Copy), reads=["PC"], writes=["kkS"])
                        for n in range(NS):
                            A("act", lambda e, n=n: e.activation(out=kkm[:, n, :], in_=kkS[:], func=AF.Copy, scale=id4[:, n, 0:1]), reads=["kkS", "id4"], writes=["kkm"])
                        for n in range(NS):
                            A("pe", lambda e, n=n, hd=hd: e.matmul(PC[:, 128 + n * 128:256 + n * 128], lhsT=kkm[:, n, :], rhs=i_tm[0:NS, NTL, hd * 128:(hd + 1) * 128], start=True, stop=True),
                              reads=["kkm", "i_tm"], writes=["PC"])
                        for n in range(NS):
                            A("dve", lambda e, n=n, m=m, hd=hd: e.scalar_tensor_tensor(out=Ssm[:, n, m, :], in0=Ssm[:, n, m, :], scalar=fS[:, hd, n:n + 1],
                                                                                     in1=PC[:, 128 + n * 128:256 + n * 128], op0=ALU.mult, op1=ALU.add),
                              reads=["Ssm%d" % n, "fS", "PC"], writes=["Ssm%d" % n])
                            A("pe", lambda e, n=n, m=m, hd=hd, PO=PO: e.matmul(PO[:, NP_ + n:NP_ + n + 1], lhsT=Ssm[:, n, m, :], rhs=qS[:, hd, n:n + 1], start=True, stop=True),
                              reads=["Ssm%d" % n, "qS"], writes=[POn])
                        A("act", lambda e, sqb=sqb, PO=PO: e.activation(out=sqb[:], in_=PO[:, 0:NT], func=AF.Square), reads=[POn], writes=[sqbn])
                        ps, pn = next_ps()
                        A("pe", lambda e, ps=ps, sqb=sqb: e.matmul(ps[:, 0:NT], lhsT=onesb[:], rhs=sqb[:], start=True, stop=True), reads=[sqbn, "onesb"], writes=[pn])
                        rstd_from(ps, pn, 1.0 / 128.0, NT, rs_, rsn)
                        A("dve", lambda e, t1=t1, PO=PO, rs_=rs_: e.tensor_tensor(out=t1[:], in0=PO[:, 0:NT], in1=rs_[:, 0:NT], op=ALU.mult), reads=[POn, rsn], writes=[t1n])
                        oi = l * 8 + hd
                        A("dve", lambda e, hd=hd, oi=oi, t1=t1: e.scalar_tensor_tensor(out=mix[:, 8 + hd, :], in0=t1[:], scalar=gout[:, oi:oi + 1], in1=sga[:, hd, :],
                                                                              op0=ALU.mult, op1=ALU.mult), reads=[t1n, "gout", "sga"], writes=["mix%d" % (8 + hd)])
                    for n in range(NS):
                        sn = g * NS + n
                        A("sp", lambda e, n=n, sn=sn, l=l, hp=hp: e.dma_start(out=Ss_out[l, sn, 2 * hp:2 * hp + 2].rearrange("h k v -> k h v"), in_=Ssm[:, n, :, :]),
                          reads=["Ssm%d" % n], dma_key="sso%d" % n)

                for t in range(8):
                    wt, wn = w_take()
                    for m in range(2):
                        mc = 2 * t + m
                        ps, pn = next_ps()
                        mm_fm(ps, pn, wt, wn, m * 128, mix, "mix", KC)
                        A("act", lambda e, ps=ps, mc=mc: e.activation(out=mixo3[:, mc, :], in_=ps[:, 0:NT], func=AF.Copy), reads=[pn], writes=["mixo%d" % mc])
                        A("act", lambda e, ps=ps, mc=mc: e.activation(out=h[:, mc, :], in_=ps[:, 0:NT], func=AF.Square), reads=[pn], writes=["h%d" % mc])
                postnorm_residual(l, 1)

                prenorm(l, 2)
                for qd in range(NQ):
                    for jj in range(JQ):
                        wt, wn = w_take()
                        psg, png = next_ps()
                        mm_fm(psg, png, wt, wn, 0, h, "h", KC)
                        psu, pnu = next_ps()
                        mm_fm(psu, pnu, wt, wn, 128, h, "h", KC)
                        A("act", lambda e, psg=psg: e.activation(out=t1[:], in_=psg[:, 0:NT], func=AF.Silu), reads=[png], writes=["t1_0"])
                        A("dve", lambda e, psu=psu, jj=jj: e.tensor_tensor(out=hid[:, jj, :], in0=t1[:], in1=psu[:, 0:NT], op=ALU.mult),
                          reads=["t1_0", pnu], writes=["hid%d" % jj])
                    for dp in range(8):
                        wt, wn = w_take()
                        for m in range(2):
                            mc = 2 * dp + m
                            ps, pn = next_ps()
                            mm_fm(ps, pn, wt, wn, m * 128, hid, "hid", JQ)
                            if qd == 0:
                                A("act", lambda e, ps=ps, mc=mc: e.activation(out=mixo3[:, mc, :], in_=ps[:, 0:NT], func=AF.Copy), reads=[pn], writes=["mixo%d" % mc])
                            else:
                                A("dve", lambda e, ps=ps, mc=mc: e.tensor_tensor(out=mixo3[:, mc, :], in0=mixo3[:, mc, :], in1=ps[:, 0:NT], op=ALU.add),
                                  reads=[pn, "mixo%d" % mc], writes=["mixo%d" % mc])
                            if qd == NQ - 1:
                                A("act", lambda e, mc=mc: e.activation(out=h[:, mc, :], in_=mixo3[:, mc, :], func=AF.Square), reads=["mixo%d" % mc], writes=["h%d" % mc])
                postnorm_residual(l, 3)
            A("sp", lambda e, c0=c0: e.dma_start(out=yT[:, :, c0:c0 + NT], in_=x[:, :, :]), reads=RA("x"), dma_key="yout")
        S.emit_all(nc, st)
    return nc


_NC_CACHE = {}


def _tile_w(w, c0, ncols):
    K = w.shape[0]
    t = w[:, c0:c0 + ncols].reshape(K // 128, 128, ncols).transpose(1, 0, 2)
    return np.ascontiguousarray(t).reshape(128, -1)


def kernel(x_prompt, x_sample, state_hgrn, norm_mix_pre, norm_mix_post, w_in, ln_v_gain, ln_v_bias,
           spatial_w, spatial_b, lb_param, hgrn_out_gain, w_out, norm_ffn_pre, norm_ffn_post,
           w_gate, w_up, w_down):
    f32 = np.float32
    x_prompt = np.asarray(x_prompt, f32)
    x_sample = np.asarray(x_sample, f32)
    state_hgrn = np.asarray(state_hgrn, f32)
    w_in = np.asarray(w_in, f32); w_out = np.asarray(w_out, f32)
    w_gate = np.asarray(w_gate, f32); w_up = np.asarray(w_up, f32); w_down = np.asarray(w_down, f32)
    n_cores = 8
    wA = np.empty((DEPTH, 24, 128, KC * 256), f32)
    wO = np.empty((DEPTH, 8, 128, KC * 256), f32)
    wGU = np.empty((DEPTH, 44, 128, KC * 256), f32)
    wD = np.empty((DEPTH, NQ * 8, 128, JQ * 256), f32)
    for l in range(DEPTH):
        cols = [1024 + 256 * j for j in range(4)]
        for hp in range(4):
            cols += [2048 + 256 * hp, 5120 + 256 * hp, 3072 + 256 * hp, 4096 + 256 * hp]
        cols += [256 * hp for hp in range(4)]
        for t, c in enumerate(cols):
            wA[l, t] = _tile_w(w_in[l], c, 256)
        for t in range(8):
            wO[l, t] = _tile_w(w_out[l], 256 * t, 256)
        for j in range(44):
            gu = np.concatenate([w_gate[l][:, 128 * j:128 * j + 128], w_up[l][:, 128 * j:128 * j + 128]], axis=1)
            wGU[l, j] = _tile_w(gu, 0, 256)
        for qd in range(NQ):
            rows = w_down[l][qd * JQ * 128:(qd + 1) * JQ * 128]
            for dp in range(8):
                wD[l, qd * 8 + dp] = _tile_w(rows, 256 * dp, 256)
    gains = np.stack([np.asarray(a, f32) for a in (norm_mix_pre, norm_mix_post, norm_ffn_pre, norm_ffn_post)], axis=1)
    gains = np.ascontiguousarray(gains.reshape(DEPTH, 4, KC, 128).transpose(3, 0, 1, 2)).reshape(128, -1)
    lnp = np.ascontiguousarray(np.stack([np.asarray(ln_v_gain, f32), np.asarray(ln_v_bias, f32)], axis=1))
    sw = np.asarray(spatial_w, f32)
    wsT = np.ascontiguousarray(sw.transpose(0, 3, 1, 2)).reshape(DEPTH, 128, 1024)
    sbias = np.asarray(spatial_b, f32)
    bs = np.ascontiguousarray(sbias.reshape(DEPTH, 1, 1024))
    bsS = np.ascontiguousarray(np.repeat(sbias[:, :, 0:1], 4, axis=2).reshape(DEPTH, 1, 32))
    wd4 = np.zeros((DEPTH, 4, 8, 4), f32)
    for p in range(4):
        wd4[:, p, :, p] = sw[:, :, 0, 0]
    wd4 = wd4.reshape(DEPTH, 4, 32)
    lbp = np.ascontiguousarray(np.asarray(lb_param, f32).reshape(DEPTH, 8, 128).transpose(2, 0, 1)).reshape(128, -1)
    gout = np.ascontiguousarray(np.asarray(hgrn_out_gain, f32).reshape(DEPTH, 8, 128).transpose(2, 0, 1)).reshape(128, -1)
    m1 = np.triu(np.ones((128, 128), f32))
    mask8 = np.ascontiguousarray(np.tile(m1[:, None, :], (1, 8, 1))).reshape(128, 1024)
    ident = np.eye(128, dtype=f32)
    id4 = np.zeros((4, 4, 128), f32)
    for p in range(4):
        id4[p, p, :] = 1.0
    id4 = id4.reshape(4, 512)

    in_maps = []
    for c in range(n_cores):
        b, par = c // 2, c % 2
        xT = np.empty((128, KC, NG * NT), f32)
        for g in range(NG):
            gi = 2 * g + par
            blk = x_prompt[b, gi * NP_:(gi + 1) * NP_, :]
            xT[:, :, g * NT:g * NT + NP_] = blk.T.reshape(KC, 128, NP_).transpose(1, 0, 2)
            sblk = x_sample[c * 16 + g * NS:c * 16 + (g + 1) * NS, 0, :]
            xT[:, :, g * NT + NP_:(g + 1) * NT] = sblk.T.reshape(KC, 128, NS).transpose(1, 0, 2)
        Sin = np.ascontiguousarray(state_hgrn[:, c * 16:(c + 1) * 16])
        flag = np.zeros((128, 2), f32)
        flag[:, par] = 1.0
        in_maps.append(dict(xT=xT, Sin=Sin, wA=wA, wO=wO, wGU=wGU, wD=wD, gains=gains, lnp=lnp, wsT=wsT, bs=bs, bsS=bsS,
                            wd4=wd4, lbp=lbp, gout=gout, mask8=mask8, ident=ident, id4=id4, flag=flag))
    if "nc" not in _NC_CACHE:
        _NC_CACHE["nc"] = build_nc()
    nc = _NC_CACHE["nc"]
    res = run_bass_kernel_spmd(nc, in_maps, core_ids=list(range(n_cores)))
    outs = res.results
    y_prompt = np.empty((4, 2048, D), f32)
    y_sample = np.empty((128, 1, D), f32)
    Sp = np.empty((DEPTH, 4, 8, 128, 128), f32)
    Ss = np.empty((DEPTH, 128, 8, 128, 128), f32)
    vp = np.empty((DEPTH, 4, 128, 1024), f32)
    vs = np.empty((DEPTH, 128, 1, 1024), f32)
    for c in range(n_cores):
        b, par = c // 2, c % 2
        o = outs[c]
        yT = np.asarray(o["yT"]).reshape(128, KC, NG * NT)
        for g in range(NG):
            gi = 2 * g + par
            sblk = yT[:, :, g * NT + NP_:(g + 1) * NT]
            y_sample[c * 16 + g * NS:c * 16 + (g + 1) * NS, 0, :] = sblk.transpose(2, 1, 0).reshape(NS, D)
            blk = yT[:, :, g * NT:g * NT + NP_]
            y_prompt[b, gi * NP_:(gi + 1) * NP_, :] = blk.transpose(2, 1, 0).reshape(NP_, D)
        Ss[:, c * 16:(c + 1) * 16] = np.asarray(o["Ss_out"]).reshape(DEPTH, 16, 8, 128, 128)
        vs[:, c * 16:(c + 1) * 16, 0, :] = np.asarray(o["vs_out"]).reshape(DEPTH, 16, 1024)
        if par == 1:
            Sp[:, b] = np.asarray(o["Sp_out"]).reshape(DEPTH, 8, 128, 128)
            vp[:, b] = np.asarray(o["vp_out"]).reshape(DEPTH, 128, 1024)
    return (y_prompt, y_sample, Sp, Ss, vp, vs)
```

```python
import numpy as np
from contextlib import ExitStack
import concourse.bass as bass
import concourse.mybir as mybir
from concourse.bass_utils import run_bass_kernel_spmd

F32 = mybir.dt.float32
BF16 = mybir.dt.bfloat16
AF = mybir.ActivationFunctionType
ALU = mybir.AluOpType

DEPTH = 4
D = 2048
KC = 16
NP_ = 256
NS = 4
NT = NP_ + NS
NG = 4
NTL = 2
NSLOT = 4
NWT = 108
DFF = 5632
EPS = 1e-6
NQ = 4
JQ = 11


class _Op:
    __slots__ = ("eng", "emit", "deps", "tok", "signal", "val")

    def __init__(self, eng, emit, deps, tok):
        self.eng, self.emit, self.deps, self.tok = eng, emit, deps, tok
        self.signal = False
        self.val = 0


class Sched:
    ENG = ["pe", "act", "dve", "pool", "sp"]

    def __init__(self):
        self.ops = {e: [] for e in self.ENG}
        self.lastw = {}
        self.readers = {}
        self.dma_count = {}
        self.dma_inc = {}

    def add(self, eng, emit, reads=(), writes=(), dma_key=None, inc=16):
        deps = set()
        for r in reads:
            t = self.lastw.get(r)
            if t is not None:
                deps.add(t)
        for w in writes:
            t = self.lastw.get(w)
            if t is not None and (t[0] == "dma" or t[0] != eng or dma_key is not None):
                deps.add(t)
            for t in self.readers.get(w, ()):
                if t[0] == "dma" or t[0] != eng or dma_key is not None:
                    deps.add(t)
        idx = len(self.ops[eng])
        if dma_key is not None:
            c = self.dma_count.get(dma_key, 0) + 1
            self.dma_count[dma_key] = c
            tok = ("dma", dma_key, inc * c, inc)
            self.dma_inc[dma_key] = inc
        else:
            tok = (eng, idx)
        if eng == "pe":
            deps = {d for d in deps if d[0] != "pe"}
        deps.discard(tok)
        op = _Op(eng, emit, deps, tok)
        self.ops[eng].append(op)
        for r in reads:
            lst = self.readers.setdefault(r, [])
            if tok[0] != "dma":
                lst[:] = [t for t in lst if t[0] != eng]
            lst.append(tok)
        for w in writes:
            self.lastw[w] = tok
            self.readers[w] = []
        return tok

    def emit_all(self, nc, st):
        for e in self.ENG:
            for op in self.ops[e]:
                for d in op.deps:
                    if d[0] != "dma":
                        self.ops[d[0]][d[1]].signal = True
        for e in self.ENG:
            c = 0
            for op in self.ops[e]:
                if op.signal:
                    c += 1
                op.val = c
        esem = {e: st.enter_context(nc.semaphore("s_" + e)) for e in self.ENG}
        dsem = {k: st.enter_context(nc.semaphore("d_" + str(k))) for k in self.dma_count}
        block = st.enter_context(nc.Block())
        engobj = {"pe": "tensor", "act": "scalar", "dve": "vector", "pool": "gpsimd", "sp": "sync"}

        def mk(e):
            def body(eng):
                waited = {}
                for op in self.ops[e]:
                    for d in sorted(op.deps, key=str):
                        if d[0] == "dma":
                            sem, v, key = dsem[d[1]], d[2], ("dma", d[1])
                        else:
                            sem, v, key = esem[d[0]], self.ops[d[0]][d[1]].val, d[0]
                        if waited.get(key, 0) >= v:
                            continue
                        waited[key] = v
                        eng.wait_ge(sem, v)
                    ins = op.emit(eng)
                    if op.tok[0] == "dma":
                        ins.then_inc(dsem[op.tok[1]], op.tok[3])
                    elif op.signal:
                        ins.then_inc(esem[e], 1)
                if e == "sp":
                    for k, c in self.dma_count.items():
                        eng.wait_ge(dsem[k], self.dma_inc[k] * c)
            return body

        for e in self.ENG:
            getattr(block, engobj[e])(mk(e))


def build_nc():
    nc = bass.Bass("TRN2", target_bir_lowering=False)
    dt = nc.dram_tensor
    xT = dt("xT", [128, KC, NG * NT], F32, kind="ExternalInput").ap()
    Sin = dt("Sin", [DEPTH, NG * NS, 8, 128, 128], F32, kind="ExternalInput").ap()
    wA = dt("wA", [DEPTH, 24, 128, KC * 256], F32, kind="ExternalInput").ap()
    wO = dt("wO", [DEPTH, 8, 128, KC * 256], F32, kind="ExternalInput").ap()
    wGU = dt("wGU", [DEPTH, 44, 128, KC * 256], F32, kind="ExternalInput").ap()
    wD = dt("wD", [DEPTH, NQ * 8, 128, JQ * 256], F32, kind="ExternalInput").ap()
    gains_d = dt("gains", [128, DEPTH * 4 * KC], F32, kind="ExternalInput").ap()
    lnp_d = dt("lnp", [DEPTH, 2, 1024], F32, kind="ExternalInput").ap()
    wsT_d = dt("wsT", [DEPTH, 128, 1024], F32, kind="ExternalInput").ap()
    bs_d = dt("bs", [DEPTH, 1, 1024], F32, kind="ExternalInput").ap()
    bsS_d = dt("bsS", [DEPTH, 1, 32], F32, kind="ExternalInput").ap()
    wd4_d = dt("wd4", [DEPTH, 4, 32], F32, kind="ExternalInput").ap()
    lbp_d = dt("lbp", [128, DEPTH * 8], F32, kind="ExternalInput").ap()
    gout_d = dt("gout", [128, DEPTH * 8], F32, kind="ExternalInput").ap()
    mask8_d = dt("mask8", [128, 256], F32, kind="ExternalInput").ap()
    ident_d = dt("ident", [128, 128], F32, kind="ExternalInput").ap()
    id4_d = dt("id4", [4, 4 * 128], F32, kind="ExternalInput").ap()
    flag_d = dt("flag", [128, 2], F32, kind="ExternalInput").ap()

    yT = dt("yT", [128, KC, NG * NT], F32, kind="ExternalOutput").ap()
    Sp_out = dt("Sp_out", [DEPTH, 8, 128, 128], F32, kind="ExternalOutput").ap()
    Ss_out = dt("Ss_out", [DEPTH, NG * NS, 8, 128, 128], F32, kind="ExternalOutput").ap()
    vp_out = dt("vp_out", [DEPTH, 128, 1024], F32, kind="ExternalOutput").ap()
    vs_out = dt("vs_out", [DEPTH, NG * NS, 1024], F32, kind="ExternalOutput").ap()
    T1 = dt("T1", [DEPTH, 128, 1024], F32, kind="Internal").ap()
    cc_in = dt("cc_in", [2 * 128, 1024], F32, kind="Internal").ap()
    cc_out = dt("cc_out", [4 * 128, 1024], F32, kind="Internal").ap()
    wbf_l = [dt("wbf%d" % l, [NWT, 128, KC * 256], BF16, kind="Internal").ap() for l in range(DEPTH)]

    S = Sched()
    A = S.add
    with ExitStack() as st:
        def sb(name, shape, dtype):
            return st.enter_context(nc.sbuf_tensor(name, shape, dtype))

        def pst(name, shape, dtype):
            return st.enter_context(nc.psum_tensor(name, shape, dtype))

        def fl(t):
            return t[:, :, :].rearrange("p a b -> p (a b)")

        def RA(name, n=KC):
            return ["%s%d" % (name, i) for i in range(n)]

        x = sb("x", [128, KC, NT], F32)
        h = sb("h", [128, KC, NT], BF16)
        mixo = sb("mixo", [128, KC * NT], F32)
        mixo3 = mixo[:, :].rearrange("p (a b) -> p a b", b=NT)
        gv = mixo[:, 0:3 * 1024].rearrange("p (a b) -> p a b", b=1024)
        wsf = mixo[:, 0:1024]
        mix = sb("mix", [128, KC, NT], BF16)
        hid = sb("hid", [128, JQ, NT], BF16)
        wring = [sb("wr%d" % i, [128, KC * 256], BF16) for i in range(NSLOT)]
        lng = sb("lng", [128, 1024], F32)
        lnb = sb("lnb", [128, 1024], F32)
        v_tm = sb("v_tm", [128, 3, 1024], BF16)
        u2 = sb("u2", [128, 2, NT], BF16)
        qa = sb("qa", [128, 8, NT], BF16)
        qS = sb("qS", [128, 8, NS], F32)
        sga = sb("sga", [128, 8, NT], BF16)
        kka = sb("kka", [128, 8, NT], F32)
        logf_2 = [sb("logf_%d" % i, [128, NT], F32) for i in range(2)]
        fS = sb("fS", [128, 8, NS], F32)
        i_tm = sb("i_tm", [128, 3, 1024], BF16)
        bc_2 = [sb("bc_%d" % i, [128, 2, 128], F32) for i in range(2)]
        bc = bc_2[0]
        eb_2 = [sb("eb_%d" % i, [128, 2, 128], F32) for i in range(2)]
        eb = eb_2[0]
        E1_2 = [sb("E1_%d" % i, [128, 2, 128], F32) for i in range(2)]
        E1 = E1_2[0]
        E2_2 = [sb("E2_%d" % i, [128, 2, 128], F32) for i in range(2)]
        E2 = E2_2[0]
        E3_2 = [sb("E3_%d" % i, [128, 2, 128], F32) for i in range(2)]
        E3 = E3_2[0]
        E4_2 = [sb("E4_%d" % i, [128, 2, 128], F32) for i in range(2)]
        E4 = E4_2[0]
        E3l = sb("E3l", [128, 8, 2], F32)
        qea = sb("qea", [128, 8, NP_], BF16)
        kea = sb("kea", [128, 8, NP_], BF16)
        qga = sb("qga", [128, 8, NP_], BF16)
        kdT_2 = [sb("kdT_%d" % i, [128, 2, 128], BF16) for i in range(2)]
        kdT = kdT_2[0]
        kd_2 = [sb("kd_%d" % i, [128, 2, 128], BF16) for i in range(2)]
        kd = kd_2[0]
        Ama = sb("Ama", [128, 8, NP_], BF16)
        Ama4 = Ama[:, :, :].rearrange("p h (c t) -> p h c t", t=128)
        dSa = sb("dSa", [128, 8, NP_], F32)
        S1t = sb("S1t", [128, 8, 128], F32)
        Sft = sb("Sft", [128, 8, 128], F32)
        Sct = sb("Sct", [128, 8, 128], F32)
        T1s = sb("T1s", [128, 8, 128], F32)
        S0b = sb("S0b", [128, 8, 128], BF16)
        S1b = sb("S1b", [128, 8, 128], BF16)
        Ssm = sb("Ssm", [128, NS, 2, 128], F32)
        kkS = sb("kkS", [4, 128], F32)
        kkm = sb("kkm", [4, NS, 128], BF16)
        sqb_2 = [sb("sqb_%d" % i, [128, NT], BF16) for i in range(2)]
        sqb = sqb_2[0]
        t1_2 = [sb("t1_%d" % i, [128, NT], F32) for i in range(2)]
        t1 = t1_2[0]
        rs_2 = [sb("rs_%d" % i, [128, NT], F32) for i in range(2)]
        rs = rs_2[0]
        stt = sb("stt", [128, 2, 6], F32)
        mv = sb("mv", [128, 2], F32)
        rsd = sb("rsd", [128, 1], F32)
        gains = sb("gains_s", [128, DEPTH * 4 * KC], F32)
        gout = sb("gout_s", [128, DEPTH * 8], F32)
        lbe = sb("lbe", [128, DEPTH * 8], F32)
        oml = sb("oml", [128, DEPTH * 8], F32)
        ltmp = sb("ltmp", [128, 8], F32)
        lr = sb("lr", [128, 8], F32)
        mask8 = sb("mask8_s", [128, 256], F32)
        mask8v = mask8[:, :].rearrange("p (h t) -> p h t", t=128)
        ident = sb("ident_s", [128, 128], F32)
        identb = sb("identb", [128, 128], BF16)
        id4 = sb("id4_s", [4, NS, 128], F32)
        onesb = sb("onesb", [128, 128], BF16)
        onesf = sb("onesf", [128, 128], F32)
        flag = sb("flag_s", [128, 2], F32)
        wsb = sb("wsb", [128, 8, 128], BF16)
        bsf = sb("bsf", [1, 1024], F32)
        bsS = sb("bsS_s", [1, 32], F32)
        wd4f = sb("wd4f", [4, 32], F32)
        wd4b = sb("wd4b", [4, 8, 4], BF16)

        PD = [(pst("PD%d" % i, [128, 512], F32), "PD%d" % i) for i in range(4)]
        PC = pst("PC", [128, 1024], F32)
        PO_2 = [pst("PO%d" % i, [128, 512], F32) for i in range(2)]
        rot = [0]

        def next_ps():
            p = PD[rot[0] % 4]
            rot[0] += 1
            return p

        wstream = []
        for g in range(NG):
            for l in range(DEPTH):
                for t in range(24):
                    wstream.append((wA[l, t], KC * 256))
                for t in range(8):
                    wstream.append((wO[l, t], KC * 256))
                for qd in range(NQ):
                    for jj in range(JQ):
                        wstream.append((wGU[l, qd * JQ + jj], KC * 256))
                    for dp in range(8):
                        wstream.append((wD[l, qd * 8 + dp], JQ * 256))
        wstate = {"issued": 0, "taken": 0}

        def w_issue_upto(n):
            while wstate["issued"] < min(n, len(wstream)):
                i = wstate["issued"]
                src, ncols = wstream[i]
                sl = i % NSLOT
                j = i % (DEPTH * NWT)
                if i < DEPTH * NWT:
                    A("pool", lambda e, sl=sl, src=src, ncols=ncols: e.dma_start(out=wring[sl][:, 0:ncols], in_=src),
                      writes=["wr%d" % sl], dma_key="wr%d" % sl)
                    A("sp", lambda e, sl=sl, j=j, ncols=ncols: e.dma_start(out=wbf_l[j // NWT][j % NWT, :, 0:ncols], in_=wring[sl][:, 0:ncols]),
                      reads=["wr%d" % sl], writes=["wbf%d" % j], dma_key="ws%d" % sl)
                else:
                    A("sp", lambda e, sl=sl, j=j, ncols=ncols: e.dma_start(out=wring[sl][:, 0:ncols], in_=wbf_l[j // NWT][j % NWT, :, 0:ncols]),
                      reads=["wbf%d" % j], writes=["wr%d" % sl], dma_key="wr%d" % sl)
                wstate["issued"] += 1

        def w_take():
            i = wstate["taken"]
            wstate["taken"] += 1
            w_issue_upto(i + NSLOT)
            sl = i % NSLOT
            return wring[sl], "wr%d" % sl

        A("sp", lambda e: e.dma_start(out=gains[:], in_=gains_d), writes=["gains"], dma_key="c0")
        A("sp", lambda e: e.dma_start(out=gout[:], in_=gout_d), writes=["gout"], dma_key="c1")
        A("sp", lambda e: e.dma_start(out=lbe[:], in_=lbp_d), writes=["lbe"], dma_key="c2")
        A("sp", lambda e: e.dma_start(out=mask8[:], in_=mask8_d), writes=["mask8"], dma_key="c3")
        A("sp", lambda e: e.dma_start(out=ident[:], in_=ident_d), writes=["ident"], dma_key="c4")
        A("sp", lambda e: e.dma_start(out=fl(id4), in_=id4_d), writes=["id4"], dma_key="c5")
        A("sp", lambda e: e.dma_start(out=flag[:], in_=flag_d), writes=["flag"], dma_key="c6")
        A("dve", lambda e: e.memset(onesb[:], 1.0), writes=["onesb"])
        A("dve", lambda e: e.memset(onesf[:], 1.0), writes=["onesf"])
        A("dve", lambda e: e.memset(fl(Ama), 0.0), writes=["Ama"])
        A("dve", lambda e: e.memset(fl(Sft), 0.0), writes=["Sft"])
        A("dve", lambda e: e.tensor_copy(out=identb[:], in_=ident[:]), reads=["ident"], writes=["identb"])
        for l in range(DEPTH):
            A("sp", lambda e, l=l: e.dma_start(out=T1[l], in_=fl(Sft)), reads=["Sft"], writes=["T1_%d" % l], dma_key="t1w")
        A("act", lambda e: e.activation(out=lbe[:], in_=lbe[:], func=AF.Exp), reads=["lbe"], writes=["lbe"])
        A("dve", lambda e: e.tensor_tensor(out=ltmp[:], in0=lbe[:, 0:8], in1=lbe[:, 8:16], op=ALU.add), reads=["lbe"], writes=["ltmp"])
        A("dve", lambda e: e.tensor_tensor(out=ltmp[:], in0=ltmp[:], in1=lbe[:, 16:24], op=ALU.add), reads=["lbe", "ltmp"], writes=["ltmp"])
        A("dve", lambda e: e.tensor_tensor(out=ltmp[:], in0=ltmp[:], in1=lbe[:, 24:32], op=ALU.add), reads=["lbe", "ltmp"], writes=["ltmp"])
        A("dve", lambda e: e.reciprocal(out=lr[:], in_=ltmp[:]), reads=["ltmp"], writes=["lr"])
        A("dve", lambda e: e.memset(oml[:, 0:8], 1.0), writes=["oml"])
        A("dve", lambda e: e.tensor_copy(out=ltmp[:], in_=lbe[:, 8:16]), reads=["lbe", "lr"], writes=["ltmp"])
        for l in range(1, DEPTH):
            if l > 1:
                A("dve", lambda e, l=l: e.tensor_tensor(out=ltmp[:], in0=ltmp[:], in1=lbe[:, 8 * l:8 * l + 8], op=ALU.add),
                  reads=["lbe", "ltmp", "oml"], writes=["ltmp"])
            A("dve", lambda e, l=l: e.tensor_tensor(out=oml[:, 8 * l:8 * l + 8], in0=ltmp[:], in1=lr[:], op=ALU.mult),
              reads=["ltmp", "lr"], writes=["oml"])
            A("dve", lambda e, l=l: e.tensor_scalar(out=oml[:, 8 * l:8 * l + 8], in0=oml[:, 8 * l:8 * l + 8], scalar1=-1.0, scalar2=1.0,
                                                  op0=ALU.mult, op1=ALU.add), reads=["oml"], writes=["oml"])

        def rstd_from(ps, pname, scale, n, rs_t=None, rs_n="rs_0"):
            rt = rs if rs_t is None else rs_t
            A("act", lambda e: e.activation(out=rt[:, 0:n], in_=ps[:, 0:n], func=AF.Sqrt, scale=scale, bias=EPS), reads=[pname], writes=[rs_n])
            A("dve", lambda e: e.reciprocal(out=rt[:, 0:n], in_=rt[:, 0:n]), reads=[rs_n], writes=[rs_n])

        def colsum_h_to_rs(nchunks, scale):
            ps, pn = next_ps()
            for kc in range(nchunks):
                A("pe", lambda e, kc=kc: e.matmul(ps[:, 0:NT], lhsT=onesb[:], rhs=h[:, kc, :], start=(kc == 0), stop=(kc == nchunks - 1)),
                  reads=["h%d" % kc, "onesb"], writes=[pn])
            rstd_from(ps, pn, scale, NT)

        def on_pool(kc):
            return False

        def prenorm(l, nidx):
            for kc in range(KC):
                A("act", lambda e, kc=kc: e.activation(out=h[:, kc, :], in_=x[:, kc, :], func=AF.Square), reads=["x%d" % kc], writes=["h%d" % kc])
            colsum_h_to_rs(KC, 1.0 / D)
            for kc in range(KC):
                gi = (l * 4 + nidx) * KC + kc
                if on_pool(kc):
                    A("pool", lambda e, kc=kc: e.tensor_tensor(out=tmpP[:], in0=x[:, kc, :], in1=rs[:, 0:NT], op=ALU.mult), reads=["x%d" % kc, "rs_0"], writes=["tmpP"])
                    A("pool", lambda e, kc=kc, gi=gi: e.tensor_scalar(out=h[:, kc, :], in0=tmpP[:], scalar1=gains[:, gi:gi + 1], scalar2=None, op0=ALU.mult),
                      reads=["tmpP", "gains"], writes=["h%d" % kc])
                else:
                    A("dve", lambda e, kc=kc, gi=gi: e.scalar_tensor_tensor(out=h[:, kc, :], in0=x[:, kc, :], scalar=gains[:, gi:gi + 1], in1=rs[:, 0:NT],
                                                                            op0=ALU.mult, op1=ALU.mult), reads=["x%d" % kc, "rs_0", "gains"], writes=["h%d" % kc])

        def postnorm_residual(l, nidx):
            colsum_h_to_rs(KC, 1.0 / D)
            for kc in range(KC):
                gi = (l * 4 + nidx) * KC + kc
                if on_pool(kc):
                    A("pool", lambda e, kc=kc, gi=gi: e.tensor_scalar(out=mixo3[:, kc, :], in0=mixo3[:, kc, :], scalar1=gains[:, gi:gi + 1], scalar2=None, op0=ALU.mult),
                      reads=["mixo%d" % kc, "gains"], writes=["mixo%d" % kc])
                    A("pool", lambda e, kc=kc: e.tensor_tensor(out=mixo3[:, kc, :], in0=mixo3[:, kc, :], in1=rs[:, 0:NT], op=ALU.mult),
                      reads=["mixo%d" % kc, "rs_0"], writes=["mixo%d" % kc])
                    A("pool", lambda e, kc=kc: e.tensor_tensor(out=x[:, kc, :], in0=x[:, kc, :], in1=mixo3[:, kc, :], op=ALU.add),
                      reads=["mixo%d" % kc, "x%d" % kc], writes=["x%d" % kc])
                else:
                    A("dve", lambda e, kc=kc, gi=gi: e.scalar_tensor_tensor(out=mixo3[:, kc, :], in0=mixo3[:, kc, :], scalar=gains[:, gi:gi + 1], in1=rs[:, 0:NT],
                                                                            op0=ALU.mult, op1=ALU.mult), reads=["mixo%d" % kc, "rs_0", "gains"], writes=["mixo%d" % kc])
                    A("dve", lambda e, kc=kc: e.tensor_tensor(out=x[:, kc, :], in0=x[:, kc, :], in1=mixo3[:, kc, :], op=ALU.add),
                      reads=["mixo%d" % kc, "x%d" % kc], writes=["x%d" % kc])

        def mm_fm(ps, pn, wt, wn, wcol, act, an, nk, wstride=256):
            for kc in range(nk):
                A("pe", lambda e, kc=kc: e.matmul(ps[:, 0:NT], lhsT=wt[:, kc * wstride + wcol: kc * wstride + wcol + 128],
                                                  rhs=act[:, kc, :], start=(kc == 0), stop=(kc == nk - 1)),
                  reads=[wn, "%s%d" % (an, kc)], writes=[pn])

        def mm_tm(ps, pn, wt, wn, tt):
            a, b = (tt * 128, tt * 128 + 128) if tt < NTL else (NP_, NT)
            nt = b - a
            for kc in range(KC):
                A("pe", lambda e, kc=kc, a=a, b=b, nt=nt: e.matmul(ps[0:nt, 0:256], lhsT=h[:, kc, a:b], rhs=wt[:, kc * 256:(kc + 1) * 256],
                                                                     start=(kc == 0), stop=(kc == KC - 1)), reads=[wn, "h%d" % kc], writes=[pn])
            return nt

        def state_apply(src, srcn, dst1, dst1n, dstf, dstfn):
            for hd in range(8):
                A("dve", lambda e, hd=hd: e.scalar_tensor_tensor(out=dst1[:, hd, :], in0=src[:, hd, :], scalar=E3l[:, hd, 0:1], in1=dSa[:, hd, 0:128],
                                                               op0=ALU.mult, op1=ALU.add), reads=[srcn, "E3l", "dSa"], writes=[dst1n])
                A("dve", lambda e, hd=hd: e.scalar_tensor_tensor(out=dstf[:, hd, :], in0=dst1[:, hd, :], scalar=E3l[:, hd, 1:2], in1=dSa[:, hd, 128:256],
                                                               op0=ALU.mult, op1=ALU.add), reads=[dst1n, "E3l", "dSa"], writes=[dstfn])

        slot = 0
        for g in range(NG):
            c0 = g * NT
            A("sp", lambda e, c0=c0: e.dma_start(out=x[:, :, :], in_=xT[:, :, c0:c0 + NT]), writes=RA("x"), dma_key="xin")
            for l in range(DEPTH):
                last_g = (g == NG - 1)
                A("sp", lambda e, l=l: e.dma_start(out=lng[:], in_=lnp_d[l, 0:1, :].partition_broadcast(128)), writes=["lng"], dma_key="p0")
                A("sp", lambda e, l=l: e.dma_start(out=lnb[:], in_=lnp_d[l, 1:2, :].partition_broadcast(128)), writes=["lnb"], dma_key="p1")
                A("sp", lambda e, l=l: e.dma_start(out=wsf, in_=wsT_d[l]), writes=RA("mixo"), dma_key="p2")
                A("sp", lambda e, l=l: e.dma_start(out=bsf[:], in_=bs_d[l]), writes=["bsf"], dma_key="p3")
                A("sp", lambda e, l=l: e.dma_start(out=bsS[:], in_=bsS_d[l]), writes=["bsS"], dma_key="p4")
                A("sp", lambda e, l=l: e.dma_start(out=wd4f[:], in_=wd4_d[l]), writes=["wd4f"], dma_key="p5")
                for q4 in range(4):
                    A("dve", lambda e, q4=q4: e.tensor_tensor(out=fl(wsb)[:, q4 * 256:(q4 + 1) * 256], in0=wsf[:, q4 * 256:(q4 + 1) * 256], in1=mask8[:], op=ALU.mult),
                      reads=RA("mixo") + ["mask8"], writes=["wsb"])
                A("dve", lambda e: e.tensor_copy(out=fl(wd4b), in_=wd4f[:]), reads=["wd4f"], writes=["wd4b"])
                A("sp", lambda e, l=l: e.dma_start(out=fl(T1s), in_=T1[l]), reads=["T1_%d" % l], writes=["T1s"], dma_key="t1r")

                prenorm(l, 0)

                for j in range(4):
                    wt, wn = w_take()
                    for tt in range(NTL + 1):
                        nt = mm_tm(PC, "PC", wt, wn, tt)
                        A("act", lambda e, tt=tt, nt=nt, j=j: e.activation(out=gv[0:nt, tt, j * 256:(j + 1) * 256], in_=PC[0:nt, 0:256], func=AF.Gelu),
                          reads=["PC"], writes=RA("mixo"))
                for tt in range(NTL + 1):
                    nt = 128 if tt < NTL else NS
                    for hh in range(2):
                        A("dve", lambda e, tt=tt, nt=nt, hh=hh: e.bn_stats(out=stt[0:nt, hh, :], in_=gv[0:nt, tt, hh * 512:(hh + 1) * 512]),
                          reads=RA("mixo"), writes=["stt"])
                    A("dve", lambda e, nt=nt: e.bn_aggr(out=mv[0:nt, :], in_=stt[0:nt, :, :].rearrange("p a b -> p (a b)")), reads=["stt"], writes=["mv"])
                    A("act", lambda e, nt=nt: e.activation(out=rsd[0:nt, :], in_=mv[0:nt, 1:2], func=AF.Sqrt, bias=EPS), reads=["mv"], writes=["rsd"])
                    A("dve", lambda e, nt=nt: e.reciprocal(out=rsd[0:nt, :], in_=rsd[0:nt, :]), reads=["rsd"], writes=["rsd"])
                    A("dve", lambda e, tt=tt, nt=nt: e.tensor_scalar(out=gv[0:nt, tt, :], in0=gv[0:nt, tt, :], scalar1=mv[0:nt, 0:1], scalar2=rsd[0:nt, 0:1],
                                                                      op0=ALU.subtract, op1=ALU.mult), reads=RA("mixo") + ["mv", "rsd"], writes=RA("mixo"))
                    A("dve", lambda e, tt=tt, nt=nt: e.tensor_tensor(out=gv[0:nt, tt, :], in0=gv[0:nt, tt, :], in1=lng[0:nt, :], op=ALU.mult),
                      reads=RA("mixo") + ["lng"], writes=RA("mixo"))
                    A("dve", lambda e, tt=tt, nt=nt: e.tensor_tensor(out=gv[0:nt, tt, :], in0=gv[0:nt, tt, :], in1=lnb[0:nt, :], op=ALU.add),
                      reads=RA("mixo") + ["lnb"], writes=RA("mixo"))
                    A("act", lambda e, tt=tt, nt=nt: e.activation(out=v_tm[0:nt, tt, :], in_=gv[0:nt, tt, :], func=AF.Copy), reads=RA("mixo"), writes=["v_tm"])
                    if tt == NTL - 1 and last_g:
                        A("sp", lambda e, l=l, tt=tt: e.dma_start(out=vp_out[l], in_=gv[:, tt, :]), reads=RA("mixo"), dma_key="vpo")
                    if tt == NTL:
                        A("sp", lambda e, l=l, g=g, tt=tt: e.dma_start(out=vs_out[l, g * NS:(g + 1) * NS, :], in_=gv[0:NS, tt, :]), reads=RA("mixo"), dma_key="vso")

                for hp in range(4):
                    wt, wn = w_take()
                    for m in range(2):
                        hd = 2 * hp + m
                        ps, pn = next_ps()
                        mm_fm(ps, pn, wt, wn, m * 128, h, "h", KC)
                        A("act", lambda e, ps=ps, hd=hd: e.activation(out=qa[:, hd, :], in_=ps[:, 0:NT], func=AF.Silu), reads=[pn], writes=["qa"])
                        A("act", lambda e, ps=ps, hd=hd: e.activation(out=qS[:, hd, :], in_=ps[:, NP_:NT], func=AF.Silu), reads=[pn], writes=["qS"])
                    wt, wn = w_take()
                    for m in range(2):
                        hd = 2 * hp + m
                        ps, pn = next_ps()
                        mm_fm(ps, pn, wt, wn, m * 128, h, "h", KC)
                        A("act", lambda e, ps=ps, hd=hd: e.activation(out=sga[:, hd, :], in_=ps[:, 0:NT], func=AF.Silu), reads=[pn], writes=["sga"])
                    wt, wn = w_take()
                    for m in range(2):
                        hd = 2 * hp + m
                        ps, pn = next_ps()
                        mm_fm(ps, pn, wt, wn, m * 128, h, "h", KC)
                        A("act", lambda e, ps=ps, hd=hd: e.activation(out=kka[:, hd, :], in_=ps[:, 0:NT], func=AF.Sigmoid, scale=-1.0), reads=[pn], writes=["kka"])
                        oi = l * 8 + hd
                        A("dve", lambda e, hd=hd, oi=oi: e.tensor_scalar(out=kka[:, hd, :], in0=kka[:, hd, :], scalar1=oml[:, oi:oi + 1], scalar2=None, op0=ALU.mult),
                          reads=["kka", "oml"], writes=["kka"])
                        A("dve", lambda e, hd=hd: e.tensor_scalar(out=fS[:, hd, :], in0=kka[:, hd, NP_:NT], scalar1=-1.0, scalar2=1.0, op0=ALU.mult, op1=ALU.add),
                          reads=["kka"], writes=["fS"])
                    wt, wn = w_take()
                    for tt in range(NTL + 1):
                        nt = mm_tm(PC, "PC", wt, wn, tt)
                        A("act", lambda e, tt=tt, nt=nt, hp=hp: e.activation(out=i_tm[0:nt, tt, hp * 256:(hp + 1) * 256], in_=PC[0:nt, 0:256], func=AF.Copy),
                          reads=["PC"], writes=["i_tm"])

                for hd in range(8):
                    p2 = hd % 2
                    bc, eb, E1, E2, E3, E4, kdT, kd = bc_2[p2], eb_2[p2], E1_2[p2], E2_2[p2], E3_2[p2], E4_2[p2], kdT_2[p2], kd_2[p2]
                    sx = "_%d" % p2
                    logf = logf_2[p2]
                    A("act", lambda e, hd=hd, logf=logf: e.activation(out=logf[:], in_=kka[:, hd, :], func=AF.Ln, scale=-1.0, bias=1.0), reads=["kka"], writes=["logf" + sx])
                    for c in range(NTL):
                        cs = slice(c * 128, (c + 1) * 128)
                        A("dve", lambda e, c=c, cs=cs, bc=bc, logf=logf: e.tensor_tensor_scan(out=bc[:, c, :], data0=onesf[:], data1=logf[:, cs], initial=0.0,
                                                                                             op0=ALU.mult, op1=ALU.add), reads=["logf" + sx, "onesf"], writes=["bc" + sx])
                    for c in range(NTL):
                        A("dve", lambda e, c=c, bc=bc, eb=eb: e.tensor_scalar(out=eb[:, c, :], in0=bc[:, c, :], scalar1=bc[:, c, 63:64], scalar2=-80.0, op0=ALU.subtract, op1=ALU.max),
                          reads=["bc" + sx], writes=["eb" + sx])
                    A("dve", lambda e, eb=eb: e.tensor_scalar(out=fl(eb), in0=fl(eb), scalar1=80.0, scalar2=None, op0=ALU.min), reads=["eb" + sx], writes=["eb" + sx])
                    A("act", lambda e, E1=E1, eb=eb: e.activation(out=fl(E1), in_=fl(eb), func=AF.Exp), reads=["eb" + sx], writes=["E1" + sx])
                    A("act", lambda e, E2=E2, eb=eb: e.activation(out=fl(E2), in_=fl(eb), func=AF.Exp, scale=-1.0), reads=["eb" + sx], writes=["E2" + sx])
                    A("act", lambda e, E3=E3, bc=bc: e.activation(out=fl(E3), in_=fl(bc), func=AF.Exp), reads=["bc" + sx], writes=["E3" + sx])
                    for c in range(NTL):
                        A("act", lambda e, c=c, E4=E4, bc=bc: e.activation(out=E4[:, c, :], in_=bc[:, c, :], func=AF.Exp, scale=-1.0, bias=bc[:, c, 127:128]),
                          reads=["bc" + sx], writes=["E4" + sx])
                    A("dve", lambda e, hd=hd, E1=E1: e.tensor_tensor(out=qea[:, hd, :], in0=qa[:, hd, 0:NP_], in1=fl(E1), op=ALU.mult), reads=["qa", "E1" + sx], writes=["qea%d" % hd])
                    A("dve", lambda e, hd=hd, E2=E2: e.tensor_tensor(out=kea[:, hd, :], in0=kka[:, hd, 0:NP_], in1=fl(E2), op=ALU.mult), reads=["kka", "E2" + sx], writes=["kea%d" % hd])
                    A("dve", lambda e, hd=hd, E3=E3: e.tensor_tensor(out=qga[:, hd, :], in0=qa[:, hd, 0:NP_], in1=fl(E3), op=ALU.mult), reads=["qa", "E3" + sx], writes=["qga%d" % hd])
                    A("dve", lambda e, hd=hd, E4=E4, kdT=kdT: e.tensor_tensor(out=fl(kdT), in0=kka[:, hd, 0:NP_], in1=fl(E4), op=ALU.mult), reads=["kka", "E4" + sx], writes=["kdT" + sx])
                    A("act", lambda e, hd=hd, E3=E3: e.activation(out=E3l[:, hd, :], in_=E3[:, :, 127], func=AF.Copy), reads=["E3" + sx], writes=["E3l"])
                    pk, pkn = next_ps()
                    for c in range(NTL):
                        A("pe", lambda e, c=c, pk=pk, kdT=kdT: e.matmul(pk[:, c * 128:(c + 1) * 128], lhsT=kdT[:, c, :], rhs=identb[:], start=True, stop=True),
                          reads=["kdT" + sx, "identb"], writes=[pkn])
                    A("act", lambda e, pk=pk, kd=kd: e.activation(out=fl(kd), in_=pk[:, 0:NP_], func=AF.Copy), reads=[pkn], writes=["kd" + sx])
                    pa, pan = next_ps()
                    for c in range(NTL):
                        o0 = c * 128
                        A("pe", lambda e, hd=hd, o0=o0, pa=pa: e.matmul(pa[0:64, o0:o0 + 64], lhsT=kea[:, hd, o0:o0 + 64], rhs=qea[:, hd, o0:o0 + 64], start=True, stop=True),
                          reads=["kea%d" % hd, "qea%d" % hd], writes=[pan])
                        A("pe", lambda e, hd=hd, o0=o0, pa=pa: e.matmul(pa[:, o0 + 64:o0 + 128], lhsT=kea[:, hd, o0:o0 + 128], rhs=qea[:, hd, o0 + 64:o0 + 128], start=True, stop=True),
                          reads=["kea%d" % hd, "qea%d" % hd], writes=[pan])
                    A("dve", lambda e, hd=hd, pa=pa: e.tensor_tensor(out=Ama4[0:64, hd, :, 0:64], in0=pa[0:64, 0:NP_].rearrange("p (c t) -> p c t", t=128)[:, :, 0:64],
                                                                      in1=mask8v[0:64, 0:NTL, 0:64], op=ALU.mult), reads=[pan, "mask8"], writes=["Ama%d" % hd])
                    A("dve", lambda e, hd=hd, pa=pa: e.tensor_tensor(out=Ama4[:, hd, :, 64:128], in0=pa[:, 0:NP_].rearrange("p (c t) -> p c t", t=128)[:, :, 64:128],
                                                                      in1=mask8v[:, 0:NTL, 64:128], op=ALU.mult), reads=[pan, "mask8"], writes=["Ama%d" % hd])
                    pd_, pdn = next_ps()
                    for c in range(NTL):
                        A("pe", lambda e, hd=hd, c=c, pd_=pd_, kd=kd: e.matmul(pd_[:, c * 128:(c + 1) * 128], lhsT=kd[:, c, :], rhs=i_tm[:, c, hd * 128:(hd + 1) * 128], start=True, stop=True),
                          reads=["kd" + sx, "i_tm"], writes=[pdn])
                    A("act", lambda e, hd=hd, pd_=pd_: e.activation(out=dSa[:, hd, :], in_=pd_[:, 0:NP_], func=AF.Copy), reads=[pdn], writes=["dSa"])

                A("sp", lambda e: e.dma_start(out=cc_in[128:256, :], in_=fl(Sft)), reads=["Sft"], writes=["cc_in"], dma_key="cci1")
                state_apply(T1s, "T1s", S1t, "S1t", Sct, "Sct")
                A("sp", lambda e: e.dma_start(out=cc_in[0:128, :], in_=fl(Sct)), reads=["Sct"], writes=["cc_in"], dma_key="cci0")
                A("pool", lambda e: e.collective_compute("AllGather", ALU.bypass, replica_groups=[[0, 1], [2, 3], [4, 5], [6, 7]], ins=[cc_in], outs=[cc_out]),
                  reads=["cc_in"], writes=["cc_out"], dma_key="cc", inc=1)
                A("sp", lambda e: e.dma_start(out=fl(Sct), in_=cc_out[0:128, :]), reads=["cc_out"], writes=["Sct"], dma_key="cco0")
                if slot > 0:
                    pl = (slot - 1) % DEPTH
                    A("sp", lambda e, pl=pl: e.dma_start(out=T1[pl], in_=cc_out[384:512, :]), reads=["cc_out"], writes=["T1_%d" % pl], dma_key="t1w")
                slot += 1

                for hp in range(4):
                    wt, wn = w_take()
                    for m in range(2):
                        hd = 2 * hp + m
                        ps, pn = next_ps()
                        mm_fm(ps, pn, wt, wn, m * 128, h, "h", KC)
                        A("act", lambda e, ps=ps, m=m: e.activation(out=u2[:, m, :], in_=ps[:, 0:NT], func=AF.Gelu), reads=[pn], writes=["u2"])
                        ps2, pn2 = next_ps()
                        for c in range(NTL):
                            A("pe", lambda e, c=c, hd=hd, ps2=ps2: e.matmul(ps2[:, c * 128:(c + 1) * 128], lhsT=v_tm[:, c, hd * 128:(hd + 1) * 128], rhs=wsb[:, hd, :],
                                                                            start=True, stop=False), reads=["v_tm", "wsb"], writes=[pn2])
                            A("pe", lambda e, c=c, hd=hd, ps2=ps2: e.matmul(ps2[:, c * 128:(c + 1) * 128], lhsT=onesf[0:1, :], rhs=bsf[0:1, hd * 128:(hd + 1) * 128],
                                                                            start=False, stop=True), reads=["onesf", "bsf"], writes=[pn2])
                        A("pe", lambda e, hd=hd, ps2=ps2: e.matmul(ps2[:, NP_:NT], lhsT=v_tm[0:NS, NTL, hd * 128:(hd + 1) * 128], rhs=wd4b[:, hd, :],
                                                                   start=True, stop=False), reads=["v_tm", "wd4b"], writes=[pn2])
                        A("pe", lambda e, hd=hd, ps2=ps2: e.matmul(ps2[:, NP_:NT], lhsT=onesf[0:1, :], rhs=bsS[0:1, hd * 4:(hd + 1) * 4],
                                                                   start=False, stop=True), reads=["onesf", "bsS"], writes=[pn2])
                        A("dve", lambda e, m=m, hd=hd, ps2=ps2: e.tensor_tensor(out=mix[:, hd, :], in0=u2[:, m, :], in1=ps2[:, 0:NT], op=ALU.mult),
                          reads=["u2", pn2], writes=["mix%d" % hd])

                A("dve", lambda e: e.tensor_scalar(out=fl(T1s), in0=fl(T1s), scalar1=flag[:, 0:1], scalar2=None, op0=ALU.mult), reads=["T1s", "flag"], writes=["T1s"])
                A("dve", lambda e: e.scalar_tensor_tensor(out=fl(Sct), in0=fl(Sct), scalar=flag[:, 1:2], in1=fl(T1s), op0=ALU.mult, op1=ALU.add),
                  reads=["Sct", "flag", "T1s"], writes=["Sct"])
                state_apply(Sct, "Sct", S1t, "S1t", Sft, "Sft")
                A("act", lambda e: e.activation(out=fl(S0b), in_=fl(Sct), func=AF.Copy), reads=["Sct"], writes=["S0b"])
                A("act", lambda e: e.activation(out=fl(S1b), in_=fl(S1t), func=AF.Copy), reads=["S1t"], writes=["S1b"])
                if last_g:
                    A("sp", lambda e, l=l: e.dma_start(out=Sp_out[l].rearrange("h k v -> k h v"), in_=Sft[:, :, :]), reads=["Sft"], dma_key="spo")

                for hp in range(4):
                    for n in range(NS):
                        sn = g * NS + n
                        A("sp", lambda e, n=n, sn=sn, l=l, hp=hp: e.dma_start(out=Ssm[:, n, :, :], in_=Sin[l, sn, 2 * hp:2 * hp + 2].rearrange("h k v -> k h v")),
                          writes=["Ssm%d" % n], dma_key="ssl%d" % n)
                    for m in range(2):
                        hd = 2 * hp + m
                        PO, POn = PO_2[m], "PO%d" % m
                        sqb, t1, rs_, t1n, sqbn, rsn = sqb_2[m], t1_2[m], rs_2[m], "t1_%d" % m, "sqb_%d" % m, "rs_%d" % m
                        for c in range(NTL):
                            cs = slice(c * 128, (c + 1) * 128)
                            Sb, Sbn = (S0b, "S0b") if c == 0 else (S1b, "S1b")
                            A("pe", lambda e, c=c, hd=hd, cs=cs, PO=PO: e.matmul(PO[:, cs], lhsT=i_tm[:, c, hd * 128:(hd + 1) * 128], rhs=Ama[:, hd, cs], start=True, stop=False),
                              reads=["i_tm", "Ama%d" % hd], writes=[POn])
                            A("pe", lambda e, hd=hd, cs=cs, Sb=Sb, PO=PO: e.matmul(PO[:, cs], lhsT=Sb[:, hd, :], rhs=qga[:, hd, cs], start=False, stop=True),
                              reads=[Sbn, "qga%d" % hd], writes=[POn])
                        A("pe", lambda e, hd=hd: e.matmul(PC[0:NS, 0:128], lhsT=kka[:, hd, NP_:NT], rhs=ident[:], start=True, stop=True), reads=["kka", "ident"], writes=["PC"])
                        A("act", lambda e: e.activation(out=kkS[:], in_=PC[0:NS, 0:128], func=AF.Copy), reads=["PC"], writes=["kkS"])
                        for n in range(NS):
                            A("act", lambda e, n=n: e.activation(out=kkm[:, n, :], in_=kkS[:], func=AF.Copy, scale=id4[:, n, 0:1]), reads=["kkS", "id4"], writes=["kkm"])
                        for n in range(NS):
                            A("pe", lambda e, n=n, hd=hd: e.matmul(PC[:, 128 + n * 128:256 + n * 128], lhsT=kkm[:, n, :], rhs=i_tm[0:NS, NTL, hd * 128:(hd + 1) * 128], start=True, stop=True),
                              reads=["kkm", "i_tm"], writes=["PC"])
                        for n in range(NS):
                            A("dve", lambda e, n=n, m=m, hd=hd: e.scalar_tensor_tensor(out=Ssm[:, n, m, :], in0=Ssm[:, n, m, :], scalar=fS[:, hd, n:n + 1],
                                                                                     in1=PC[:, 128 + n * 128:256 + n * 128], op0=ALU.mult, op1=ALU.add),
                              reads=["Ssm%d" % n, "fS", "PC"], writes=["Ssm%d" % n])
                            A("pe", lambda e, n=n, m=m, hd=hd, PO=PO: e.matmul(PO[:, NP_ + n:NP_ + n + 1], lhsT=Ssm[:, n, m, :], rhs=qS[:, hd, n:n + 1], start=True, stop=True),
                              reads=["Ssm%d" % n, "qS"], writes=[POn])
                        A("act", lambda e, sqb=sqb, PO=PO: e.activation(out=sqb[:], in_=PO[:, 0:NT], func=AF.Square), reads=[POn], writes=[sqbn])
                        ps, pn = next_ps()
                        A("pe", lambda e, ps=ps, sqb=sqb: e.matmul(ps[:, 0:NT], lhsT=onesb[:], rhs=sqb[:], start=True, stop=True), reads=[sqbn, "onesb"], writes=[pn])
                        rstd_from(ps, pn, 1.0 / 128.0, NT, rs_, rsn)
                        A("dve", lambda e, t1=t1, PO=PO, rs_=rs_: e.tensor_tensor(out=t1[:], in0=PO[:, 0:NT], in1=rs_[:, 0:NT], op=ALU.mult), reads=[POn, rsn], writes=[t1n])
                        oi = l * 8 + hd
                        A("dve", lambda e, hd=hd, oi=oi, t1=t1: e.scalar_tensor_tensor(out=mix[:, 8 + hd, :], in0=t1[:], scalar=gout[:, oi:oi + 1], in1=sga[:, hd, :],
                                                                              op0=ALU.mult, op1=ALU.mult), reads=[t1n, "gout", "sga"], writes=["mix%d" % (8 + hd)])
                    for n in range(NS):
                        sn = g * NS + n
                        A("sp", lambda e, n=n, sn=sn, l=l, hp=hp: e.dma_start(out=Ss_out[l, sn, 2 * hp:2 * hp + 2].rearrange("h k v -> k h v"), in_=Ssm[:, n, :, :]),
                          reads=["Ssm%d" % n], dma_key="sso%d" % n)

                for t in range(8):
                    wt, wn = w_take()
                    for m in range(2):
                        mc = 2 * t + m
                        ps, pn = next_ps()
                        mm_fm(ps, pn, wt, wn, m * 128, mix, "mix", KC)
                        A("act", lambda e, ps=ps, mc=mc: e.activation(out=mixo3[:, mc, :], in_=ps[:, 0:NT], func=AF.Copy), reads=[pn], writes=["mixo%d" % mc])
                        A("act", lambda e, ps=ps, mc=mc: e.activation(out=h[:, mc, :], in_=ps[:, 0:NT], func=AF.Square), reads=[pn], writes=["h%d" % mc])
                postnorm_residual(l, 1)

                prenorm(l, 2)
                for qd in range(NQ):
                    for jj in range(JQ):
                        wt, wn = w_take()
                        psg, png = next_ps()
                        mm_fm(psg, png, wt, wn, 0, h, "h", KC)
                        psu, pnu = next_ps()
                        mm_fm(psu, pnu, wt, wn, 128, h, "h", KC)
                        A("act", lambda e, psg=psg: e.activation(out=t1[:], in_=psg[:, 0:NT], func=AF.Silu), reads=[png], writes=["t1_0"])
                        A("dve", lambda e, psu=psu, jj=jj: e.tensor_tensor(out=hid[:, jj, :], in0=t1[:], in1=psu[:, 0:NT], op=ALU.mult),
                          reads=["t1_0", pnu], writes=["hid%d" % jj])
                    for dp in range(8):
                        wt, wn = w_take()
                        for m in range(2):
                            mc = 2 * dp + m
                            ps, pn = next_ps()
                            mm_fm(ps, pn, wt, wn, m * 128, hid, "hid", JQ)
                            if qd == 0:
                                A("act", lambda e, ps=ps, mc=mc: e.activation(out=mixo3[:, mc, :], in_=ps[:, 0:NT], func=AF.Copy), reads=[pn], writes=["mixo%d" % mc])
                            else:
                                A("dve", lambda e, ps=ps, mc=mc: e.tensor_tensor(out=mixo3[:, mc, :], in0=mixo3[:, mc, :], in1=ps[:, 0:NT], op=ALU.add),
                                  reads=[pn, "mixo%d" % mc], writes=["mixo%d" % mc])
                            if qd == NQ - 1:
                                A("act", lambda e, mc=mc: e.activation(out=h[:, mc, :], in_=mixo3[:, mc, :], func=AF.Square), reads=["mixo%d" % mc], writes=["h%d" % mc])
                postnorm_residual(l, 3)
            A("sp", lambda e, c0=c0: e.dma_start(out=yT[:, :, c0:c0 + NT], in_=x[:, :, :]), reads=RA("x"), dma_key="yout")
        S.emit_all(nc, st)
    return nc


_NC_CACHE = {}


def _tile_w(w, c0, ncols):
    K = w.shape[0]
    t = w[:, c0:c0 + ncols].reshape(K // 128, 128, ncols).transpose(1, 0, 2)
    return np.ascontiguousarray(t).reshape(128, -1)


def kernel(x_prompt, x_sample, state_hgrn, norm_mix_pre, norm_mix_post, w_in, ln_v_gain, ln_v_bias,
           spatial_w, spatial_b, lb_param, hgrn_out_gain, w_out, norm_ffn_pre, norm_ffn_post,
           w_gate, w_up, w_down):
    f32 = np.float32
    x_prompt = np.asarray(x_prompt, f32)
    x_sample = np.asarray(x_sample, f32)
    state_hgrn = np.asarray(state_hgrn, f32)
    w_in = np.asarray(w_in, f32); w_out = np.asarray(w_out, f32)
    w_gate = np.asarray(w_gate, f32); w_up = np.asarray(w_up, f32); w_down = np.asarray(w_down, f32)
    n_cores = 8
    wA = np.empty((DEPTH, 24, 128, KC * 256), f32)
    wO = np.empty((DEPTH, 8, 128, KC * 256), f32)
    wGU = np.empty((DEPTH, 44, 128, KC * 256), f32)
    wD = np.empty((DEPTH, NQ * 8, 128, JQ * 256), f32)
    for l in range(DEPTH):
        cols = [1024 + 256 * j for j in range(4)]
        for hp in range(4):
            cols += [2048 + 256 * hp, 5120 + 256 * hp, 3072 + 256 * hp, 4096 + 256 * hp]
        cols += [256 * hp for hp in range(4)]
        for t, c in enumerate(cols):
            wA[l, t] = _tile_w(w_in[l], c, 256)
        for t in range(8):
            wO[l, t] = _tile_w(w_out[l], 256 * t, 256)
        for j in range(44):
            gu = np.concatenate([w_gate[l][:, 128 * j:128 * j + 128], w_up[l][:, 128 * j:128 * j + 128]], axis=1)
            wGU[l, j] = _tile_w(gu, 0, 256)
        for qd in range(NQ):
            rows = w_down[l][qd * JQ * 128:(qd + 1) * JQ * 128]
            for dp in range(8):
                wD[l, qd * 8 + dp] = _tile_w(rows, 256 * dp, 256)
    gains = np.stack([np.asarray(a, f32) for a in (norm_mix_pre, norm_mix_post, norm_ffn_pre, norm_ffn_post)], axis=1)
    gains = np.ascontiguousarray(gains.reshape(DEPTH, 4, KC, 128).transpose(3, 0, 1, 2)).reshape(128, -1)
    lnp = np.ascontiguousarray(np.stack([np.asarray(ln_v_gain, f32), np.asarray(ln_v_bias, f32)], axis=1))
    sw = np.asarray(spatial_w, f32)
    wsT = np.ascontiguousarray(sw.transpose(0, 3, 1, 2)).reshape(DEPTH, 128, 1024)
    sbias = np.asarray(spatial_b, f32)
    bs = np.ascontiguousarray(sbias.reshape(DEPTH, 1, 1024))
    bsS = np.ascontiguousarray(np.repeat(sbias[:, :, 0:1], 4, axis=2).reshape(DEPTH, 1, 32))
    wd4 = np.zeros((DEPTH, 4, 8, 4), f32)
    for p in range(4):
        wd4[:, p, :, p] = sw[:, :, 0, 0]
    wd4 = wd4.reshape(DEPTH, 4, 32)
    lbp = np.ascontiguousarray(np.asarray(lb_param, f32).reshape(DEPTH, 8, 128).transpose(2, 0, 1)).reshape(128, -1)
    gout = np.ascontiguousarray(np.asarray(hgrn_out_gain, f32).reshape(DEPTH, 8, 128).transpose(2, 0, 1)).reshape(128, -1)
    m1 = np.triu(np.ones((128, 128), f32))
    mask8 = np.ascontiguousarray(np.tile(m1[:, None, :], (1, 2, 1))).reshape(128, 256)
    ident = np.eye(128, dtype=f32)
    id4 = np.zeros((4, 4, 128), f32)
    for p in range(4):
        id4[p, p, :] = 1.0
    id4 = id4.reshape(4, 512)

    in_maps = []
    for c in range(n_cores):
        b, par = c // 2, c % 2
        xT = np.empty((128, KC, NG * NT), f32)
        for g in range(NG):
            gi = 2 * g + par
            blk = x_prompt[b, gi * NP_:(gi + 1) * NP_, :]
            xT[:, :, g * NT:g * NT + NP_] = blk.T.reshape(KC, 128, NP_).transpose(1, 0, 2)
            sblk = x_sample[c * 16 + g * NS:c * 16 + (g + 1) * NS, 0, :]
            xT[:, :, g * NT + NP_:(g + 1) * NT] = sblk.T.reshape(KC, 128, NS).transpose(1, 0, 2)
        Sin = np.ascontiguousarray(state_hgrn[:, c * 16:(c + 1) * 16])
        flag = np.zeros((128, 2), f32)
        flag[:, par] = 1.0
        in_maps.append(dict(xT=xT, Sin=Sin, wA=wA, wO=wO, wGU=wGU, wD=wD, gains=gains, lnp=lnp, wsT=wsT, bs=bs, bsS=bsS,
                            wd4=wd4, lbp=lbp, gout=gout, mask8=mask8, ident=ident, id4=id4, flag=flag))
    if "nc" not in _NC_CACHE:
        _NC_CACHE["nc"] = build_nc()
    nc = _NC_CACHE["nc"]
    res = run_bass_kernel_spmd(nc, in_maps, core_ids=list(range(n_cores)))
    outs = res.results
    y_prompt = np.empty((4, 2048, D), f32)
    y_sample = np.empty((128, 1, D), f32)
    Sp = np.empty((DEPTH, 4, 8, 128, 128), f32)
    Ss = np.empty((DEPTH, 128, 8, 128, 128), f32)
    vp = np.empty((DEPTH, 4, 128, 1024), f32)
    vs = np.empty((DEPTH, 128, 1, 1024), f32)
    for c in range(n_cores):
        b, par = c // 2, c % 2
        o = outs[c]
        yT = np.asarray(o["yT"]).reshape(128, KC, NG * NT)
        for g in range(NG):
            gi = 2 * g + par
            sblk = yT[:, :, g * NT + NP_:(g + 1) * NT]
            y_sample[c * 16 + g * NS:c * 16 + (g + 1) * NS, 0, :] = sblk.transpose(2, 1, 0).reshape(NS, D)
            blk = yT[:, :, g * NT:g * NT + NP_]
            y_prompt[b, gi * NP_:(gi + 1) * NP_, :] = blk.transpose(2, 1, 0).reshape(NP_, D)
        Ss[:, c * 16:(c + 1) * 16] = np.asarray(o["Ss_out"]).reshape(DEPTH, 16, 8, 128, 128)
        vs[:, c * 16:(c + 1) * 16, 0, :] = np.asarray(o["vs_out"]).reshape(DEPTH, 16, 1024)
        if par == 1:
            Sp[:, b] = np.asarray(o["Sp_out"]).reshape(DEPTH, 8, 128, 128)
            vp[:, b] = np.asarray(o["vp_out"]).reshape(DEPTH, 128, 1024)
    return (y_prompt, y_sample, Sp, Ss, vp, vs)
```
